# Optimizing a Trainium2 kernel written in Bass

```python
import jax, jax.numpy as jnp
from jax import lax
import numpy as np

D_MODEL = 1024
BATCH = 16
SEQ = 256
DEPTH = 1
DEC_BATCH = 2
DEC_SEQ = 1024
PAST_LEN = 512

GRID_W = 64
D_MIX = D_MODEL
D_A = D_MIX // 2
D_B = D_MIX - D_A
H_A = 4
H_B = 4
DV_A = D_A // H_A
DK_A = DV_A // 2
DV_B = D_B // H_B
DK_B = DV_B // 2
R_ALPHA = 16
TAU_GLA = 16.0
CONV_K = 3
CHUNK = 64
D_FF = 4 * D_MODEL
EPS = 1e-6
SPLIT_SIZES = (H_A * DK_A, H_A * DK_A, D_A, D_A, 2 * R_ALPHA,
               H_B * DK_B, H_B * DK_B, D_B, D_B, 4 * H_B)
D_IN = sum(SPLIT_SIZES)

kernel_name = "hybrid_gla_mlstm_diffusion_step"


def rmsnorm(x, w):
    x32 = x.astype(jnp.float32)
    y = x32 * lax.rsqrt(jnp.mean(x32 * x32, axis=-1, keepdims=True) + EPS)
    return (y * w.astype(jnp.float32)).astype(x.dtype)


def head_rmsnorm(o, w):
    y = o * lax.rsqrt(jnp.mean(o * o, axis=-1, keepdims=True) + EPS)
    B, T, H, dv = o.shape
    return y.reshape(B, T, H * dv) * w.astype(jnp.float32)


def rev(a):
    return jnp.flip(a, axis=1)


def to_chunks(a):
    B, T, H = a.shape[:3]
    N = T // CHUNK
    if a.ndim == 4:
        return a.reshape(B, N, CHUNK, H, a.shape[-1]).transpose(1, 0, 3, 2, 4)
    return a.reshape(B, N, CHUNK, H).transpose(1, 0, 3, 2)


def from_chunks(o):
    N, B, H, L, d = o.shape
    return o.transpose(1, 0, 3, 2, 4).reshape(B, N * L, H, d)


def grid_dwconv(u, w, rows):
    B, T, C = u.shape
    img = u.reshape(B, rows, T // rows, C)
    out = lax.conv_general_dilated(img, w[:, :, None, :].astype(u.dtype), window_strides=(1, 1), padding='SAME',
                                   dimension_numbers=('NHWC', 'HWIO', 'NHWC'), feature_group_count=C)
    return out.reshape(B, T, C)


def gla_scan(q, k, v, g, S0):
    mask = jnp.tril(jnp.ones((CHUNK, CHUNK), dtype=bool))

    def step(S, inp):
        qc, kc, vc, gc = inp
        b = jnp.cumsum(gc, axis=2)
        diff = jnp.where(mask[:, :, None], b[:, :, :, None, :] - b[:, :, None, :, :], -jnp.inf)
        scores = jnp.einsum('bhtd,bhsd,bhtsd->bhts', qc, kc, jnp.exp(diff))
        o = jnp.einsum('bhts,bhsv->bhtv', scores, vc) + jnp.einsum('bhtd,bhdv->bhtv', qc * jnp.exp(b), S)
        bL = b[:, :, -1]
        S = jnp.exp(bL)[..., None] * S + jnp.einsum('bhsd,bhsv->bhdv', kc * jnp.exp(bL[:, :, None] - b), vc)
        return S, o

    S, o = lax.scan(step, S0.astype(jnp.float32), (to_chunks(q), to_chunks(k), to_chunks(v), to_chunks(g)))
    return from_chunks(o), S


def mlstm_scan(q, k, v, ig, lf, C0, n0, m0):
    mask = jnp.tril(jnp.ones((CHUNK, CHUNK), dtype=bool))

    def step(carry, inp):
        C, n, m = carry
        qc, kc, vc, ic, fc = inp
        F = jnp.cumsum(fc, axis=-1)
        D = jnp.where(mask, F[..., :, None] - F[..., None, :] + ic[..., None, :], -jnp.inf)
        inter = F + m[..., None]
        mt = jnp.maximum(inter, jnp.max(D, axis=-1))
        w_inter = jnp.exp(inter - mt)
        s = jnp.einsum('bhtd,bhsd->bhts', qc, kc) * jnp.exp(D - mt[..., None])
        num = jnp.einsum('bhts,bhsv->bhtv', s, vc) + w_inter[..., None] * jnp.einsum('bhtd,bhdv->bhtv', qc, C)
        den = jnp.sum(s, axis=-1) + w_inter * jnp.einsum('bhtd,bhd->bht', qc, n)
        h = num / jnp.maximum(jnp.abs(den), jnp.exp(-mt))[..., None]
        FL = F[..., -1]
        m_new = mt[..., -1]
        a = jnp.exp(FL + m - m_new)
        wk = jnp.exp(FL[..., None] - F + ic - m_new[..., None])
        C = a[..., None, None] * C + jnp.einsum('bhs,bhsd,bhsv->bhdv', wk, kc, vc)
        n = a[..., None] * n + jnp.einsum('bhs,bhsd->bhd', wk, kc)
        return (C, n, m_new), h

    f32 = jnp.float32
    (C, n, m), h = lax.scan(step, (C0.astype(f32), n0.astype(f32), m0.astype(f32)),
                            (to_chunks(q), to_chunks(k), to_chunks(v), to_chunks(ig), to_chunks(lf)))
    return from_chunks(h), C, n, m


def mixer(h, rows, S_gla, C_m, n_m, m_m, w_in, w_alpha2, b_alpha, b_mgate, conv_w, gnorm_a_w, gnorm_b_w, w_out):
    f32 = jnp.float32
    B, T, _ = h.shape
    z = h @ w_in
    qa, ka, va, ga, ra, qb, kb, vb, ob, gb = jnp.split(z, np.cumsum(SPLIT_SIZES)[:-1].tolist(), axis=-1)
    qa = qa.astype(f32).reshape(B, T, H_A, DK_A) * (DK_A ** -0.5)
    ka = ka.astype(f32).reshape(B, T, H_A, DK_A)
    va = va.astype(f32).reshape(B, T, H_A, DV_A)
    ra = ra.astype(f32).reshape(B, T, 2, R_ALPHA)
    loga = jax.nn.log_sigmoid(jnp.einsum('btzr,zrk->btzk', ra, w_alpha2.astype(f32)) + b_alpha.astype(f32)) / TAU_GLA
    loga = loga.reshape(B, T, 2, H_A, DK_A)
    oa_f, Sa_f = gla_scan(qa, ka, va, loga[:, :, 0], S_gla[:, 0])
    oa_b, Sa_b = gla_scan(rev(qa), rev(ka), rev(va), rev(loga[:, :, 1]), S_gla[:, 1])
    out_a = head_rmsnorm(oa_f + rev(oa_b), gnorm_a_w) * jax.nn.silu(ga.astype(f32))
    qk = jax.nn.silu(grid_dwconv(jnp.concatenate([qb, kb], axis=-1), conv_w, rows)).astype(f32)
    qb, kb = jnp.split(qk, 2, axis=-1)
    qb = qb.reshape(B, T, H_B, DK_B) * (DK_B ** -0.5)
    kb = kb.reshape(B, T, H_B, DK_B)
    vb = vb.astype(f32).reshape(B, T, H_B, DV_B)
    gates = gb.astype(f32).reshape(B, T, 4, H_B) + b_mgate.astype(f32)
    hb_f, C_f, n_f, m_f = mlstm_scan(qb, kb, vb, gates[:, :, 0], jax.nn.log_sigmoid(gates[:, :, 1]),
                                     C_m[:, 0], n_m[:, 0], m_m[:, 0])
    hb_b, C_b, n_b, m_b = mlstm_scan(rev(qb), rev(kb), rev(vb), rev(gates[:, :, 2]),
                                     rev(jax.nn.log_sigmoid(gates[:, :, 3])), C_m[:, 1], n_m[:, 1], m_m[:, 1])
    out_b = head_rmsnorm(hb_f + rev(hb_b), gnorm_b_w) * jax.nn.sigmoid(ob.astype(f32))
    out = jnp.concatenate([out_a, out_b], axis=-1).astype(h.dtype) @ w_out
    new_states = (jnp.stack([Sa_f, Sa_b], axis=1), jnp.stack([C_f, C_b], axis=1),
                  jnp.stack([n_f, n_b], axis=1), jnp.stack([m_f, m_b], axis=1))
    return out, new_states


def block(x, cond, rows, states, w_ada, b_ada, norm1_w, norm2_w, mix_params, w_ff1, w_ff2):
    mod = (jax.nn.silu(cond) @ w_ada + b_ada)[:, None, :]
    sh1, sc1, g1, sh2, sc2, g2 = jnp.split(mod.astype(x.dtype), 6, axis=-1)
    h = rmsnorm(x, norm1_w) * (1 + sc1) + sh1
    y, new_states = mixer(h, rows, *states, *mix_params)
    x = x + g1 * y
    h = rmsnorm(x, norm2_w) * (1 + sc2) + sh2
    x = x + g2 * (jnp.square(jax.nn.relu(h @ w_ff1)) @ w_ff2)
    return x, new_states


def setup_inputs(seed: int = 0) -> dict:
    key = jax.random.key(seed)
    ks = jax.random.split(key, 24)
    f32 = jnp.float32

    def nrm(k, shape, scale):
        return jax.random.normal(k, shape, f32) * scale

    gate_offset = jnp.array([0.0, 3.0, 0.0, 3.0], f32)[None, :, None]
    return {
        "x_prompt": nrm(ks[0], (BATCH, SEQ, D_MODEL), 1.0),
        "x_sample": nrm(ks[1], (DEC_BATCH, DEC_SEQ, D_MODEL), 1.0),
        "c": nrm(ks[2], (DEC_BATCH, D_MODEL), 1.0),
        "state_gla": nrm(ks[3], (DEC_BATCH, DEPTH, 2, H_A, DK_A, DV_A), 0.5),
        "state_mlstm_C": nrm(ks[4], (DEC_BATCH, DEPTH, 2, H_B, DK_B, DV_B), 0.5),
        "state_mlstm_n": jnp.abs(nrm(ks[5], (DEC_BATCH, DEPTH, 2, H_B, DK_B), 0.5)),
        "state_mlstm_m": nrm(ks[6], (DEC_BATCH, DEPTH, 2, H_B), 0.5),
        "c_ctx": nrm(ks[7], (D_MODEL,), 1.0),
        "w_ada": nrm(ks[8], (DEPTH, D_MODEL, 6 * D_MODEL), 0.5 * D_MODEL ** -0.5),
        "b_ada": nrm(ks[9], (DEPTH, 6 * D_MODEL), 0.02),
        "norm1_w": 1.0 + nrm(ks[10], (DEPTH, D_MODEL), 0.02),
        "norm2_w": 1.0 + nrm(ks[11], (DEPTH, D_MODEL), 0.02),
        "w_in": nrm(ks[12], (DEPTH, D_MODEL, D_IN), D_MODEL ** -0.5),
        "w_alpha2": nrm(ks[13], (DEPTH, 2, R_ALPHA, H_A * DK_A), R_ALPHA ** -0.5),
        "b_alpha": nrm(ks[14], (DEPTH, 2, H_A * DK_A), 0.1),
        "b_mgate": gate_offset + nrm(ks[15], (DEPTH, 4, H_B), 0.1),
        "conv_w": nrm(ks[16], (DEPTH, CONV_K, CONV_K, 2 * H_B * DK_B), 1.0 / CONV_K),
        "gnorm_a_w": 1.0 + nrm(ks[17], (DEPTH, D_A), 0.02),
        "gnorm_b_w": 1.0 + nrm(ks[18], (DEPTH, D_B), 0.02),
        "w_out": nrm(ks[19], (DEPTH, D_MIX, D_MODEL), D_MIX ** -0.5),
        "w_ff1": nrm(ks[20], (DEPTH, D_MODEL, D_FF), D_MODEL ** -0.5),
        "w_ff2": nrm(ks[21], (DEPTH, D_FF, D_MODEL), D_FF ** -0.5),
        "final_norm_w": 1.0 + nrm(ks[22], (D_MODEL,), 0.02),
    }


def reference(x_prompt, x_sample, c, state_gla, state_mlstm_C, state_mlstm_n, state_mlstm_m, c_ctx,
              w_ada, b_ada, norm1_w, norm2_w, w_in, w_alpha2, b_alpha, b_mgate, conv_w,
              gnorm_a_w, gnorm_b_w, w_out, w_ff1, w_ff2, final_norm_w):
    f32 = jnp.float32
    Bp = x_prompt.shape[0]
    rows_lat = x_sample.shape[1] // GRID_W
    zero_states = (jnp.zeros((Bp, 2, H_A, DK_A, DV_A), f32), jnp.zeros((Bp, 2, H_B, DK_B, DV_B), f32),
                   jnp.zeros((Bp, 2, H_B, DK_B), f32), jnp.zeros((Bp, 2, H_B), f32))
    xp = x_prompt
    xs = x_sample
    s_gla, s_C, s_n, s_m = [], [], [], []
    for l in range(DEPTH):
        mix_params = (w_in[l], w_alpha2[l], b_alpha[l], b_mgate[l], conv_w[l], gnorm_a_w[l], gnorm_b_w[l], w_out[l])
        xp, ctx_states = block(xp, c_ctx[None, :], 1, zero_states, w_ada[l], b_ada[l], norm1_w[l], norm2_w[l],
                               mix_params, w_ff1[l], w_ff2[l])
        s_gla.append(ctx_states[0]); s_C.append(ctx_states[1]); s_n.append(ctx_states[2]); s_m.append(ctx_states[3])
        cached = (state_gla[:, l], state_mlstm_C[:, l], state_mlstm_n[:, l], state_mlstm_m[:, l])
        xs, _ = block(xs, c, rows_lat, cached, w_ada[l], b_ada[l], norm1_w[l], norm2_w[l],
                      mix_params, w_ff1[l], w_ff2[l])
    y_prompt = rmsnorm(xp, final_norm_w)
    y_sample = rmsnorm(xs, final_norm_w)
    dt = x_prompt.dtype
    new_state_gla = jnp.stack(s_gla, axis=1).astype(dt)
    new_state_mlstm_C = jnp.stack(s_C, axis=1).astype(dt)
    new_state_mlstm_n = jnp.stack(s_n, axis=1).astype(dt)
    new_state_mlstm_m = jnp.stack(s_m, axis=1).astype(dt)
    return (y_prompt, y_sample, new_state_gla, new_state_mlstm_C, new_state_mlstm_n, new_state_mlstm_m)
```

```python
import numpy as np
from contextlib import ExitStack
import concourse.bass as bass
import concourse.mybir as mybir
from concourse.bass_utils import run_bass_kernel_spmd

F32 = mybir.dt.float32
BF16 = mybir.dt.bfloat16
AF = mybir.ActivationFunctionType
ALU = mybir.AluOpType
AX = mybir.AxisListType
ENGS = ("pe", "act", "dve", "pool", "sp")
NCORES = 8
BIG = 1.0e30


def _dsize(dt):
    return mybir.dt.size(dt)


class Sched:
    def __init__(self, nc, esems, dsems, base_of):
        self.nc = nc
        self.esem = esems
        self.dsems = dsems
        nq = len(dsems) // 2
        self.dpool = {"sp": list(range(0, nq)), "pool": list(range(nq, len(dsems))), "act": list(range(0, nq))}
        self.ops = []
        self.rec = {}
        self.base_of = base_of
        self.same_engine_sync = True
        self.phase = "init"
        self.reorder = True
        self.tracked_dram = set()

    def _box(self, a):
        if str(a.space) == "DRAM":
            off = int(a.offset); sz = _dsize(a.dtype)
            lo = off; hi = off
            for st, cn in a.ap:
                d = st * (cn - 1)
                if d < 0:
                    lo += d
                else:
                    hi += d
            return ("D_" + a.tensor.name, 0, 1, lo * sz, (hi + 1) * sz, ((lo * sz, (hi + 1) * sz),))
        ap = a.ap
        pstep, pcnt = ap[0]
        off = int(a.offset)
        sz = _dsize(a.dtype)
        if pstep > 0:
            p0 = off // pstep
            fo = off % pstep
        else:
            p0 = 0
            fo = off
        lo = fo
        hi = fo
        for st, cn in ap[1:]:
            d = st * (cn - 1)
            if d < 0:
                lo += d
            else:
                hi += d
        key, base = self.base_of[a.tensor.name]
        if key[0] == "P":
            return (key, (p0 // 32) * 32, ((p0 + pcnt + 31) // 32) * 32, 0, 1 << 30, ((0, 1 << 30),))
        dims = [(st, cn) for st, cn in ap[1:] if cn > 1]
        rows = None
        if len(dims) >= 2 and all(st > 0 for st, cn in dims):
            ist, icn = dims[-1]
            ilen = (icn - 1) * ist + 1
            nouter = 1
            for st, cn in dims[:-1]:
                nouter *= cn
            if nouter <= 1024:
                offs = [0]
                for st, cn in dims[:-1]:
                    offs = [o + st * i for o in offs for i in range(cn)]
                iv = sorted((base + (fo + o) * sz, base + (fo + o + ilen) * sz) for o in offs)
                merged = [list(iv[0])]
                for s_, e_ in iv[1:]:
                    if s_ <= merged[-1][1]:
                        if e_ > merged[-1][1]:
                            merged[-1][1] = e_
                    else:
                        merged.append([s_, e_])
                rows = tuple((x[0], x[1]) for x in merged)
        if rows is None:
            rows = ((base + lo * sz, base + (hi + 1) * sz),)
        return (key, p0, p0 + pcnt, base + lo * sz, base + (hi + 1) * sz, rows)

    @staticmethod
    def _ovl(b1, b2):
        if not (b1[1] < b2[2] and b2[1] < b1[2] and b1[3] < b2[4] and b2[3] < b1[4]):
            return False
        r1, r2 = b1[5], b2[5]
        if len(r1) == 1 and len(r2) == 1:
            return True
        i = j = 0
        while i < len(r1) and j < len(r2):
            if r1[i][0] < r2[j][1] and r2[j][0] < r1[i][1]:
                return True
            if r1[i][1] <= r2[j][1]:
                i += 1
            else:
                j += 1
        return False

    @staticmethod
    def _contains(b1, b2):
        if not (b1[1] <= b2[1] and b2[2] <= b1[2] and b1[3] <= b2[3] and b2[4] <= b1[4]):
            return False
        r1, r2 = b1[5], b2[5]
        i = 0
        for s_, e_ in r2:
            while i < len(r1) and r1[i][1] < e_:
                i += 1
            if i >= len(r1) or not (r1[i][0] <= s_ and e_ <= r1[i][1]):
                return False
        return True

    def _deps(self, reads, writes):
        deps = set()
        trk = lambda a: str(a.space) != "DRAM" or a.tensor.name in self.tracked_dram
        rb = [self._box(a) for a in reads if trk(a)]
        wb = [self._box(a) for a in writes if trk(a)]
        for b in rb:
            ps = b[0][0] == "P"
            for r in self.rec.get(b[0], ()):
                if (r[1] == "w" or ps) and self._ovl(b, r[0]):
                    deps.add(r[2])
        for b in wb:
            for r in self.rec.get(b[0], ()):
                if self._ovl(b, r[0]):
                    deps.add(r[2])
        return deps, rb, wb

    def _commit(self, rb, wb, oid):
        for b in wb:
            lst = self.rec.setdefault(b[0], [])
            lst[:] = [r for r in lst if not self._contains(b, r[0])]
            lst.append([b, "w", oid])
        for b in rb:
            lst = self.rec.setdefault(b[0], [])
            lst.append([b, "r", oid])

    def op(self, eng, fn, reads=(), writes=(), dur=0.2):
        deps, rb, wb = self._deps(reads, writes)
        oid = len(self.ops)
        self.ops.append(dict(eng=eng, kind="c", fn=fn, deps=deps, dur=dur, phase=self.phase))
        self._commit(rb, wb, oid)

    def dma(self, q, out, in_, **kw):
        deps, rb, wb = self._deps([in_], [out])
        oid = len(self.ops)
        nbytes = 1
        for d in out.shape:
            nbytes *= d
        nbytes *= max(_dsize(out.dtype), _dsize(in_.dtype))

        def fn(e, out=out, in_=in_, kw=kw):
            return e.dma_start(out=out, in_=in_, **kw)

        self.ops.append(dict(eng=q, kind="d", fn=fn, deps=deps, dur=0.1, bytes=nbytes, phase=self.phase,
                             isout=str(out.space) == "DRAM"))
        self._commit(rb, wb, oid)

    def _schedule(self):
        ops = self.ops
        n = len(ops)
        succ = [[] for _ in range(n)]
        for i, o in enumerate(ops):
            for d in o["deps"]:
                succ[d].append(i)
        lat = [(o["dur"] if o["kind"] == "c" else 3.0 + o["bytes"] / 200e3) for o in ops]
        rank = [0.0] * n
        for i in range(n - 1, -1, -1):
            r = 0.0
            for j in succ[i]:
                if rank[j] > r:
                    r = rank[j]
            rank[i] = r + lat[i]
        if not self.reorder:
            return {e: [i for i, o in enumerate(ops) if o["eng"] == e] for e in ENGS}, 0.0
        import heapq
        ndep = [len(o["deps"]) for o in ops]
        fin = [None] * n
        ready_t = [0.0] * n
        cand = {e: [] for e in ENGS}
        fifo = {e: [i for i, o in enumerate(ops) if o["eng"] == e and o["kind"] == "d"] for e in ENGS}
        fpos = {e: 0 for e in ENGS}
        for i, o in enumerate(ops):
            if ndep[i] == 0:
                cand[o["eng"]].append(i)
        efree = {e: 0.0 for e in ENGS}
        dma_free = {e: 0.0 for e in ENGS}
        order = {e: [] for e in ENGS}
        done = 0
        while done < n:
            best = None
            for e in ENGS:
                head = fifo[e][fpos[e]] if fpos[e] < len(fifo[e]) else -1
                for i in cand[e]:
                    if ops[i]["kind"] == "d" and i != head:
                        continue
                    st = max(efree[e], ready_t[i])
                    key = (st, -rank[i], i)
                    if best is None or key < best[0]:
                        best = (key, e, i)
            assert best is not None, "scheduler stuck"
            (st, _, _), e, i = best
            o = ops[i]
            o["st"] = st
            cand[e].remove(i)
            if o["kind"] == "d":
                fpos[e] += 1
                efree[e] = st + o["dur"]
                dstart = max(st + 1.0, dma_free[e])
                dma_free[e] = dstart + o["bytes"] / 230e3
                fin[i] = dma_free[e] + 1.5
            else:
                efree[e] = st + o["dur"]
                fin[i] = efree[e]
            order[e].append(i)
            done += 1
            for j in succ[i]:
                ndep[j] -= 1
                if fin[i] > ready_t[j]:
                    ready_t[j] = fin[i]
                if ndep[j] == 0:
                    cand[ops[j]["eng"]].append(j)
        return order, max(f for f in fin if f is not None)

    def emit(self, block):
        ops = self.ops
        order, est = self._schedule()
        self.est_makespan = est
        self.order = order
        self.labels = {e: [ops[i]["phase"] for i in order[e] if ops[i]["kind"] == "c"] for e in ENGS}
        sem_of = [None] * len(ops)
        cnt = {e: 0 for e in ENGS}
        duse = [0] * len(self.dsems)
        dnext = {e: 0 for e in ENGS}
        prog = {e: [] for e in ENGS}
        known = {e: {} for e in ENGS}
        prev_on_slot = {}
        for e in ENGS:
            for i in order[e]:
                o = ops[i]
                if o["kind"] == "c":
                    cnt[e] += 1
                    sem_of[i] = (("e", e), cnt[e])
                else:
                    pl = self.dpool[e]
                    sl = pl[dnext[e] % len(pl)]
                    dnext[e] += 1
                    duse[sl] += 1
                    sem_of[i] = (("d", sl), 16 * duse[sl])
                    o["slotprev"] = prev_on_slot.get(sl)
                    prev_on_slot[sl] = i
        out_waits = {}
        for e in ENGS:
            for i in order[e]:
                o = ops[i]
                need = {}
                deps = set(o["deps"])
                if o["kind"] == "d" and o["slotprev"] is not None:
                    deps.add(o["slotprev"])
                for d in deps:
                    kk, v = sem_of[d]
                    if need.get(kk, 0) < v:
                        need[kk] = v
                ws = []
                kn = known[e]
                for kk, v in need.items():
                    if kk == ("e", e) and (e == "pe" or not self.same_engine_sync):
                        continue
                    if kn.get(kk, 0) >= v:
                        continue
                    kn[kk] = v
                    ws.append((kk, v))
                sk, val = sem_of[i]
                prog[e].append((ws, o["fn"], sk, 1 if o["kind"] == "c" else 16))
                if o["kind"] == "d" and o.get("isout"):
                    out_waits[sk] = max(out_waits.get(sk, 0), val)
        prog["sp"].append((list(out_waits.items()), None, None, 0))
        self.prog = prog

        def _sem(k):
            return self.esem[k[1]] if k[0] == "e" else self.dsems[k[1]]

        def make(ename):
            def body(eng):
                for ws, fn, sk, inc in prog[ename]:
                    for kk, v in ws:
                        eng.wait_ge(_sem(kk), v)
                    if fn is None:
                        continue
                    fn(eng).then_inc(_sem(sk), inc)
            return body

        block.tensor(make("pe"))
        block.scalar(make("act"))
        block.vector(make("dve"))
        block.gpsimd(make("pool"))
        block.sync(make("sp"))


W_QA, W_KA, W_VA, W_GA, W_RA, W_QB, W_KB, W_VB, W_OB, W_GB = 0, 256, 512, 1024, 1536, 1568, 1824, 2080, 2592, 3104
NTOK = 1536
NOWN = 768


class K:
    def __init__(self, taps=()):
        self.taps = list(taps)
        self.nc = bass.Bass("TRN2", target_bir_lowering=False)
        self.base_of = {}
        self.sb_off = 16640
        self.st = ExitStack()
        self.dram_in = {}
        self.dram_out = {}
        self.tapouts = {}

    def sb(self, name, shape, dt, off=None):
        sz = int(np.prod(shape[1:])) * _dsize(dt)
        if off is None:
            off = self.sb_off
            self.sb_off = (off + sz + 63) // 64 * 64
        assert off + sz <= 229312, (name, off, sz)
        h = self.nc.alloc_sbuf_tensor_at(name, list(shape), dt, offset=off)
        self.base_of[h.name] = ("SB", off)
        return h

    def din(self, name, shape, dt=F32):
        t = self.nc.dram_tensor(name, list(shape), dt, kind="ExternalInput").ap()
        self.dram_in[name] = t
        return t

    def dout(self, name, shape, dt=F32):
        t = self.nc.dram_tensor(name, list(shape), dt, kind="ExternalOutput").ap()
        self.dram_out[name] = t
        return t

    @staticmethod
    def _n(ap):
        n = 1
        for d in ap.shape[1:]:
            n *= d
        return n

    def _vdur(self, eng, n, k=1.0):
        if eng == "pool":
            return 0.3 + k * n / 500.0
        if eng == "act":
            return 0.25 + k * n / 1200.0
        return 0.19 + k * n / 960.0

    def mm(self, out, lhsT, rhs, start, stop, extra_reads=(), **kw):
        n = self._n(rhs)
        d = 0.03 + max(64, n) / 2400.0 * (4 if rhs.dtype == F32 else 1)
        self.S.op("pe", lambda e: e.matmul(out, lhsT=lhsT, rhs=rhs, start=start, stop=stop, skip_group_check=True, **kw),
                  reads=[lhsT, rhs] + list(extra_reads) + ([] if start else [out]), writes=[out], dur=d)

    def tr(self, out, in_, ident):
        self.S.op("pe", lambda e: e.transpose(out=out, in_=in_, identity=ident), reads=[in_, ident], writes=[out], dur=0.09)

    def act(self, out, in_, func, bias=None, scale=1.0, accum=None, eng="act"):
        rd = [in_]
        kw = {}
        if bias is not None:
            kw["bias"] = bias
            if not isinstance(bias, (int, float)):
                rd.append(bias)
        if not isinstance(scale, (int, float)):
            rd.append(scale)
        wr = [out]
        if accum is not None:
            kw["accum_out"] = accum
            wr.append(accum)
        self.S.op("act", lambda e: e.activation(out=out, in_=in_, func=func, scale=scale, **kw), reads=rd, writes=wr,
                  dur=self._vdur("act", self._n(in_)) + (0.1 if accum is not None else 0.0))

    def tt(self, eng, out, a, b, op):
        self.S.op(eng, lambda e: e.tensor_tensor(out=out, in0=a, in1=b, op=op), reads=[a, b], writes=[out],
                  dur=self._vdur(eng, self._n(a)))

    def ts(self, eng, out, a, s1, op0, s2=None, op1=None, accum=None):
        rd = [a] + [s for s in (s1, s2) if s is not None and not isinstance(s, (int, float))]
        wr = [out] + ([accum] if accum is not None else [])
        kw = {}
        if op1 is not None:
            kw["op1"] = op1
        if accum is not None:
            kw["accum_out"] = accum
        self.S.op(eng, lambda e: e.tensor_scalar(out=out, in0=a, scalar1=s1, scalar2=s2, op0=op0, **kw), reads=rd, writes=wr,
                  dur=self._vdur(eng, self._n(a)))

    def stt(self, out, in0, scalar, in1, op0, op1):
        rd = [in0, in1] + ([] if isinstance(scalar, (int, float)) else [scalar])
        self.S.op("dve", lambda e: e.scalar_tensor_tensor(out=out, in0=in0, scalar=scalar, in1=in1, op0=op0, op1=op1),
                  reads=rd, writes=[out], dur=self._vdur("dve", self._n(in0)))

    def cp(self, eng, out, in_):
        if eng == "act":
            self.S.op("act", lambda e: e.copy(out=out, in_=in_), reads=[in_], writes=[out], dur=self._vdur("act", self._n(in_)))
        else:
            self.S.op(eng, lambda e: e.tensor_copy(out=out, in_=in_), reads=[in_], writes=[out], dur=self._vdur(eng, self._n(in_)))

    def memset(self, eng, out, v):
        self.S.op(eng, lambda e: e.memset(out, v), writes=[out], dur=self._vdur(eng, self._n(out)))

    def scan(self, out, d0, d1, init, op0, op1):
        rd = [d0, d1] + ([] if isinstance(init, (int, float)) else [init])
        self.S.op("dve", lambda e: e.tensor_tensor_scan(out=out, data0=d0, data1=d1, initial=init, op0=op0, op1=op1),
                  reads=rd, writes=[out], dur=self._vdur("dve", self._n(d1), 2.0))

    def reduce(self, out, in_, op, axis=AX.X):
        self.S.op("dve", lambda e: e.tensor_reduce(out=out, in_=in_, axis=axis, op=op), reads=[in_], writes=[out],
                  dur=self._vdur("dve", self._n(in_)))

    def recip(self, out, in_):
        self.S.op("dve", lambda e: e.reciprocal(out=out, in_=in_), reads=[in_], writes=[out], dur=self._vdur("dve", self._n(in_)))

    def dma(self, q, out, in_):
        self.S.dma(q, out, in_)

    def tap(self, name, ap):
        if name not in self.taps:
            return
        shp = list(ap.shape)
        t = self.dout("tap_" + name, shp, ap.dtype)
        self.dma("sp", t, ap)
        self.tapouts[name] = "tap_" + name


def build(taps=(), stage=99):
    k = K(taps)
    nc = k.nc
    st = k.st
    xin = k.din("xin", [NTOK, 1024])
    cond = k.din("cond", [2, 1024])
    sgla = k.din("sgla", [2, 4, 64, 128]); sC = k.din("sC", [2, 4, 64, 128])
    sn = k.din("sn", [2, 4, 64]); sm = k.din("sm", [2, 4])
    tmask = k.din("tmask", [2, 768])
    tmaskT = k.din("tmaskT", [128, 6, 2])
    cmask = k.din("cmask", [2, 1024])
    consts = k.din("consts", [128, 1024])
    w_ada = k.din("w_ada", [1024, 6144]); b_ada = k.din("b_ada", [6144])
    norm1_w = k.din("norm1_w", [1024]); norm2_w = k.din("norm2_w", [1024])
    w_in = k.din("w_in", [1024, 3120])
    w_alpha2 = k.din("w_alpha2", [2, 16, 256]); b_alpha = k.din("b_alpha", [2, 256])
    b_mgate = k.din("b_mgate", [4, 4]); conv_w = k.din("conv_w", [3, 3, 512])
    gnorm_a_w = k.din("gnorm_a_w", [512]); gnorm_b_w = k.din("gnorm_b_w", [512])
    w_out = k.din("w_out", [1024, 1024]); w_ff1 = k.din("w_ff1", [1024, 4096]); w_ff2 = k.din("w_ff2", [4096, 1024])
    final_norm_w = k.din("final_norm_w", [1024])
    yo = k.dout("yo", [NOWN, 1024])
    oS = k.dout("oS", [2, 2, 4, 64, 128]); oC = k.dout("oC", [2, 2, 4, 64, 128])
    on = k.dout("on", [2, 2, 4, 64]); om = k.dout("om", [2, 2, 4])

    PB = []
    for i in range(8):
        h = nc.alloc_psum_tensor("pb%d" % i, [128, 512], F32)
        k.base_of[h.name] = ("PS%d" % i, 0)
        PB.append(h)

    def pbf(i):
        return PB[i][:].bitcast(BF16)

    esems = {e: st.enter_context(nc.semaphore("s_" + e)) for e in ENGS}
    dsems = [st.enter_context(nc.semaphore("d%d" % i)) for i in range(40)]
    S = Sched(nc, esems, dsems, k.base_of)
    k.S = S
    wq = "pool"

    ident_f = k.sb("ident_f", [128, 128], F32)
    ident_b = k.sb("ident_b", [128, 128], BF16)
    cst = k.sb("cst", [128, 1024], F32)
    ones1 = k.sb("ones1", [128, 1], F32)
    eps_t = k.sb("eps_t", [128, 1], F32)
    xres = k.sb("xres", [128, 6, 1024], F32)
    modT = k.sb("modT", [128, 4, 8, 2], F32)
    gb_row = k.sb("gb_row", [128, 2, 2, 1024], F32)
    ssq = k.sb("ssq", [128, 16], F32)
    rstd = k.sb("rstd", [128, 16], F32)
    scondT = k.sb("scondT", [128, 8, 2], BF16)
    g1T = k.sb("g1T", [128, 8], F32); g2T = k.sb("g2T", [128, 8], F32)
    selc = k.sb("selc", [2, 2, 128], F32)
    hT_off = k.sb_off
    hT = k.sb("hT", [128, 8, NTOK], BF16)
    offA = k.sb_off
    k.memset("pool", ident_f[:], 1.0)
    S.op("pool", lambda e: e.affine_select(out=ident_f[:], in_=ident_f[:], pattern=[[-1, 128]], compare_op=ALU.is_equal,
                                           fill=0.0, base=0, channel_multiplier=1), reads=[ident_f[:]], writes=[ident_f[:]])
    k.cp("dve", ident_b[:], ident_f[:])
    k.memset("pool", ones1[:], 1.0)
    k.memset("pool", eps_t[:], 1e-6)
    k.dma("sp", cst[:], consts)
    maskU = cst[:, 0:128]; maskL = cst[:, 128:256]

    def ones_b(p, n):
        return ones1[0:p, 0:1].to_broadcast([p, n])

    S.phase = "B"
    condT = k.sb("condT", [128, 8, 2], F32)
    badd = k.sb("badd", [2, 1024], F32)
    modrow = [k.sb("modrow%d" % i, [2, 1024], F32) for i in range(2)]
    wada = [k.sb("wada%d" % i, [128, 8, 512], BF16) for i in range(3)]
    for c in range(2):
        k.dma("sp", condT[:, :, c], cond[c].rearrange("(kc p) -> p kc", p=128))
    k.dma("sp", g1T[:], norm1_w.rearrange("(kc p) -> p kc", p=128))
    k.dma("sp", g2T[:], norm2_w.rearrange("(kc p) -> p kc", p=128))
    k.act(scondT[:], condT[:], AF.Silu)
    k.cp("dve", selc[:, 0, :], ident_f[0:2, 0:1].to_broadcast([2, 128]))
    k.cp("dve", selc[:, 1, :], ident_f[0:2, 1:2].to_broadcast([2, 128]))
    def mod_piece(piece, wbufs, mr, badd, hw=512, banks=(0, 1, 2, 3), pin=()):
        pTl = PB[banks[2]][:, 256:320].rearrange("p (a b c) -> p a b c", a=4, b=8)
        for hh in range(1024 // hw):
            k.dma("sp", badd[:, hh * hw:(hh + 1) * hw] if badd.shape[1] == 1024 else badd[:, 0:hw],
                  b_ada[piece * 1024 + hh * hw: piece * 1024 + (hh + 1) * hw].partition_broadcast(2))
            wb = wbufs[(2 * (piece - 2) + hh) % len(wbufs)] if piece >= 2 else wbufs[(2 * piece + hh) % len(wbufs)]
            k.dma(wq, wb[:], w_ada[:, piece * 1024 + hh * hw: piece * 1024 + (hh + 1) * hw].rearrange("(kc p) n -> p kc n", p=128))
            ps = PB[banks[hh % 2]][0:2, 0:hw]
            for kc in range(8):
                k.mm(ps, scondT[:, kc, :], wb[:, kc, :], kc == 0, kc == 7, extra_reads=pin if kc == 0 else ())
            k.tt("dve", mr[:, hh * hw:(hh + 1) * hw], ps, badd[:, hh * hw:(hh + 1) * hw] if badd.shape[1] == 1024 else badd[:, 0:hw], ALU.add)
        if piece in (0, 1, 3, 4):
            a = (0, 1, None, 2, 3)[piece]
            for kc in range(8):
                k.tr(pTl[:, a, kc, :], mr[:, kc * 128:(kc + 1) * 128], ident_f[0:2, 0:2])
            k.cp("dve", modT[:, a], pTl[:, a])
            if a in (1, 3):
                for c in range(2):
                    k.stt(modT[:, a, :, c], modT[:, a, :, c], 1.0, (g1T if a == 1 else g2T)[:], ALU.add, ALU.mult)
        else:
            gi = 0 if piece == 2 else 1
            for c in range(2):
                for hh in range(2):
                    ps = PB[banks[3]][:, :]
                    k.mm(ps, selc[:, c, :], mr[:, hh * 512:(hh + 1) * 512], True, True)
                    k.cp("act", gb_row[:, gi, c, hh * 512:(hh + 1) * 512], ps)


    for piece in range(2):
        mod_piece(piece, wada, modrow[piece % 2], badd)
    k.tap("modT", modT[:])
    k.tap("gb_row", gb_row[:])
    if stage <= 1:
        return k

    S.phase = "C"
    xtmp = [k.sb("xtmp%d" % i, [128, 1024], F32) for i in range(3)]
    junk = k.sb("junk", [128, 1024], BF16)
    xn = [k.sb("xn%d" % i, [128, 1024], BF16) for i in range(3)]

    def norm_to_T(xt, si, dstT, tcol, c, a_sh, a_sc, par):
        k.act(junk[:], xt, AF.Square, accum=ssq[:, si:si + 1])
        k.act(rstd[:, si:si + 1], ssq[:, si:si + 1], AF.Sqrt, bias=eps_t[:, 0:1], scale=1.0 / 1024)
        k.recip(rstd[:, si:si + 1], rstd[:, si:si + 1])
        xnb = xn[par]
        k.ts("dve", xnb[:], xt, rstd[:, si:si + 1], ALU.mult)
        bk = ((4, 5), (6, 7), (2, 3))[par]
        pta = pbf(bk[0]).rearrange("p (a b) -> p a b", b=128)
        ptb = pbf(bk[1]).rearrange("p (a b) -> p a b", b=128)
        for kc in range(8):
            k.tr((pta if kc < 4 else ptb)[:, kc % 4, :], xnb[:, kc * 128:(kc + 1) * 128], ident_b[:])
        for kc in range(8):
            if kc < 4:
                k.ts("dve", dstT[:, kc, tcol:tcol + 128], pta[:, kc, :], modT[:, a_sc, kc, c:c + 1], ALU.mult,
                     modT[:, a_sh, kc, c:c + 1], ALU.add)
            else:
                k.act(dstT[:, kc, tcol:tcol + 128], ptb[:, kc - 4, :], AF.Identity, bias=modT[:, a_sh, kc, c:c + 1],
                      scale=modT[:, a_sc, kc, c:c + 1])

    for t in range(12):
        xt = xtmp[t % 3][:]
        k.dma("sp", xt, xin[t * 128:(t + 1) * 128, :])
        norm_to_T(xt, t, hT, t * 128, 0 if t < 4 else 1, 0, 1, t % 3)
    k.tap("hT", hT[:])
    if stage <= 2:
        return k

    k.sb_off = offA
    wst_buf = [k.sb("wblk%d" % i, [128, 8, 512], BF16) for i in range(2)]
    xr_off0 = k.base_of[xres.name][1]
    gb_off0 = k.base_of[gb_row.name][1]
    wst_buf = wst_buf + [k.sb("wblkx%d" % i, [128, 8, 512], BF16, off=gb_off0 + 8192 * i) for i in range(2)]
    wcnt = [0]

    def load_w(src, c0, ncols, buf=None):
        if buf is None:
            b = wst_buf[wcnt[0] % 2]; wcnt[0] += 1
        else:
            b = wst_buf[buf]
        k.dma(wq, b[:, :, 0:ncols], src[:, c0:c0 + ncols].rearrange("(kc p) n -> p kc n", p=128))
        return b

    kT = k.sb("kT", [128, 2, 2, NTOK], BF16)
    qTh = k.sb("qTh", [128, 2, 2, 2, NOWN], BF16)
    v_tok = k.sb("v_tok", [128, 12, 512], BF16)
    vb_aug = k.sb("vb_aug", [128, 12, 4, 130], BF16)
    Ga = k.sb("Ga", [128, 6, 512], BF16)
    Gb = k.sb("Gb", [128, 6, 512], BF16)
    kbTe = k.sb("kbTe", [128, 2, 2, NTOK], BF16)
    qbTh = k.sb("qbTh", [128, 2, 2, NOWN], BF16)
    ebl = k.sb("ebl", [128, 2, 2, 24], F32)
    lbTok = k.sb("lbTok", [128, 6, 36], F32)
    lb8 = k.sb("lb8", [128, 6, 8], F32)
    wbc = k.sb("wbc", [128, 4, 24], F32)
    wbc8 = k.sb("wbc8", [128, 4, 24], F32)
    tmk = k.sb("tmk", [128, 6, 2], F32)
    Mt = k.sb("Mt", [36, 24], F32); mprev = k.sb("mprev", [36, 24], F32); mnew = k.sb("mnew", [36, 24], F32)
    wst = k.sb("wst", [36, 24], F32)
    offB = k.sb_off
    wra = k.sb("wra", [128, 8, 48], BF16)
    wgI = k.sb("wgI", [128, 8, 36], BF16)
    wgF = k.sb("wgF", [128, 8, 36], BF16)
    wa2 = k.sb("wa2", [48, 256], BF16)
    raT = k.sb("raT", [48, NTOK], BF16)
    nba = k.sb("nba", [128, 2, 2], F32)
    bI = k.sb("bI", [36, 1], F32); nbF = k.sb("nbF", [36, 1], F32)
    gnwa = k.sb("gnwa", [128, 512], F32); gnwb = k.sb("gnwb", [128, 512], F32)
    offC = k.sb_off
    k.memset("dve", wra[:], 0.0); k.memset("dve", wgI[:], 0.0); k.memset("dve", wgF[:], 0.0)
    k.memset("dve", wa2[:], 0.0); k.memset("dve", bI[:], 0.0); k.memset("dve", nbF[:], 0.0)
    wv = lambda c0, n: w_in[:, c0:c0 + n].rearrange("(kc p) n -> p kc n", p=128)
    k.dma(wq, wra[:, :, 0:16], wv(W_RA, 16)); k.dma(wq, wra[:, :, 32:48], wv(W_RA + 16, 16))
    k.dma(wq, wgI[:, :, 0:4], wv(W_GB, 4)); k.dma(wq, wgI[:, :, 32:36], wv(W_GB + 8, 4))
    k.dma(wq, wgF[:, :, 0:4], wv(W_GB + 4, 4)); k.dma(wq, wgF[:, :, 32:36], wv(W_GB + 12, 4))
    k.dma(wq, wa2[0:16, :], w_alpha2[0]); k.dma(wq, wa2[32:48, :], w_alpha2[1])
    k.dma("sp", nba[:], b_alpha.rearrange("z (ft p) -> p z ft", p=128))
    k.ts("dve", nba[:], nba[:], -1.0, ALU.mult)
    col = lambda v: v.rearrange("(h o) -> h o", o=1)
    k.dma("sp", bI[0:4, :], col(b_mgate[0])); k.dma("sp", bI[32:36, :], col(b_mgate[2]))
    k.dma("sp", nbF[0:4, :], col(b_mgate[1])); k.dma("sp", nbF[32:36, :], col(b_mgate[3]))
    k.ts("dve", nbF[:], nbF[:], -1.0, ALU.mult)
    k.dma("sp", gnwa[:], gnorm_a_w.partition_broadcast(128)); k.dma("sp", gnwb[:], gnorm_b_w.partition_broadcast(128))
    k.dma("sp", tmk[:], tmaskT)
    k.memset("pool", vb_aug[:, :, :, 128:130], 1.0)
    k.memset("pool", qTh[:], 0.0)
    k.memset("pool", qbTh[:], 0.0)

    TB = [(0, 512), (512, 512), (1024, 512)]
    OB = [(0, 512), (512, 256)]
    bank = [0]

    def nextbank():
        b = bank[0]; bank[0] = (b + 1) % 4
        return PB[b]

    def fproj(lhs_of_kc, M, t0, n):
        ps = nextbank()[0:M, 0:n]
        for kc in range(8):
            k.mm(ps, lhs_of_kc(kc), hT[:, kc, t0:t0 + n], kc == 0, kc == 7)
        return ps

    S.phase = "D1"
    k.sb_off = offC
    iT = k.sb("iT", [36, NTOK], F32)
    lfp = k.sb("lfp", [36, NTOK], F32)
    for (t0, n) in TB:
        ps = fproj(lambda kc: wra[:, kc, :], 48, t0, n)
        k.cp("act", raT[:, t0:t0 + n], ps)
        ps = fproj(lambda kc: wgI[:, kc, :], 36, t0, n)
        k.act(iT[:, t0:t0 + n], ps, AF.Identity, bias=bI[:, 0:1])
        ps = fproj(lambda kc: wgF[:, kc, :], 36, t0, n)
        k.act(lfp[:, t0:t0 + n], ps, AF.Exp, bias=nbF[:, 0:1], scale=-1.0)
    k.act(lfp[:], lfp[:], AF.Ln, bias=1.0)
    k.tap("raT", raT[:]); k.tap("iT", iT[:]); k.tap("lfp", lfp[:])
    if stage <= 3:
        return k
    offD = k.sb_off
    d4_w = {"va": load_w(w_in, W_VA, 512, buf=2), "vb": load_w(w_in, W_VB, 512, buf=3)}

    S.phase = "D5"
    k.sb_off = offD
    gm = k.sb("gm", [36, 2, 768], BF16)
    Gg = k.sb("Gg", [36, NTOK], F32)
    Xs = k.sb("Xs", [36, NTOK], F32)
    Ggprev = k.sb("Ggprev", [36, 24, 1], F32)
    totp = k.sb("totp", [36, 24], F32)
    amax = k.sb("amax", [36, 24], F32)
    dirm = k.sb("dirm", [36, 1], F32)
    smT = k.sb("smT", [36, 1], F32)
    zero2 = k.sb("zero2", [36, 2], F32)
    Tinc = k.sb("Tinc", [36, 16], F32); Tprv = k.sb("Tprv", [36, 16], F32)
    bsc = k.sb("bsc", [36, 16], F32); mpr = k.sb("mpr", [36, 16], F32)
    k.memset("pool", gm[:, 0, :], 1.0)
    k.dma(wq, gm[0:4, 0, :], tmask[0].partition_broadcast(4))
    k.dma(wq, gm[32:36, 0, :], tmask[1].partition_broadcast(4))
    k.ts("dve", gm[:, 1, :], gm[:, 0, :], -1.0, ALU.add, BIG, ALU.mult)
    k.memset("pool", dirm[:], 0.0); k.memset("pool", dirm[32:36, :], 1.0)
    k.memset("pool", zero2[:], 0.0); k.memset("pool", smT[:], 0.0)
    k.dma("sp", smT[0:4, :], col(sm[0])); k.dma("sp", smT[32:36, :], col(sm[1]))
    k.memset("pool", Ggprev[:, 0:1, :], 0.0)
    k.memset("pool", Mt[:], 0.0); k.memset("pool", wst[:], 0.0); k.memset("pool", mnew[:], 0.0)
    k.tt("dve", lfp[:, 768:], lfp[:, 768:], gm[:, 0, :], ALU.mult)
    k.scan(Gg[:], ones_b(36, NTOK), lfp[:], 0.0, ALU.mult, ALU.add)
    Gg3 = Gg[:].rearrange("p (c j) -> p c j", j=64)
    X3 = Xs[:].rearrange("p (c j) -> p c j", j=64)
    lf3 = lfp[:].rearrange("p (c j) -> p c j", j=64)
    i3 = iT[:].rearrange("p (c j) -> p c j", j=64)
    Ggend = Gg3[:, :, 63:64]
    k.cp("pool", Ggprev[:, 1:24, :], Ggend[:, 0:23, :])
    k.tt("dve", totp[:], Ggend[:, :, 0], Ggprev[:, :, 0], ALU.subtract)
    k.tt("dve", X3, lf3, Gg3, ALU.subtract)
    k.tt("dve", X3, X3, Ggend.to_broadcast([36, 24, 64]), ALU.add)
    k.tt("dve", Gg3, Gg3, Ggprev[:].to_broadcast([36, 24, 64]), ALU.subtract)
    k.tt("dve", Xs[:], Xs[:], Gg[:], ALU.subtract)
    k.stt(Gg[:], Xs[:], dirm[:, 0:1], Gg[:], ALU.mult, ALU.add)
    k.tt("dve", iT[:], iT[:], Gg[:], ALU.add)
    k.tt("dve", iT[:, 768:], iT[:, 768:], gm[:, 0, :], ALU.mult)
    k.tt("dve", iT[:, 768:], iT[:, 768:], gm[:, 1, :], ALU.add)
    k.reduce(amax[:], i3, ALU.max)
    vP = lambda tl: tl[:, 0:8].rearrange("p (s c) -> p s c", c=4)
    for rows, fwd in ((slice(0, 4), True), (slice(32, 36), False)):
        prev = zero2[rows, :]
        for c in (range(4) if fwd else range(3, -1, -1)):
            k.tt("dve", vP(Mt)[rows, :, c], prev, vP(amax)[rows, :, c], ALU.max)
            k.tt("dve", vP(wst)[rows, :, c], prev, vP(Mt)[rows, :, c], ALU.subtract)
            k.tt("dve", vP(mnew)[rows, :, c], vP(Mt)[rows, :, c], vP(totp)[rows, :, c], ALU.subtract)
            prev = vP(mnew)[rows, :, c]
        prev = smT[rows, :]
        for (lo, hi) in ((12, 24), (8, 12)):
            n_ = hi - lo
            wv = (lambda tl: tl[rows, lo:hi]) if fwd else (lambda tl: tl[rows, lo:hi][:, ::-1])
            k.scan(Tinc[rows, 0:n_], ones_b(36, n_)[rows, :], wv(totp), 0.0, ALU.mult, ALU.add)
            k.tt("dve", Tprv[rows, 0:n_], Tinc[rows, 0:n_], wv(totp), ALU.subtract)
            k.tt("dve", bsc[rows, 0:n_], wv(amax), Tprv[rows, 0:n_], ALU.add)
            k.scan(mpr[rows, 1:n_ + 1], bsc[rows, 0:n_], bsc[rows, 0:n_], prev, ALU.max, ALU.max)
            k.cp("dve", mpr[rows, 0:1], prev)
            k.tt("dve", wv(Mt), mpr[rows, 1:n_ + 1], Tprv[rows, 0:n_], ALU.subtract)
            k.tt("dve", wv(mnew), mpr[rows, 1:n_ + 1], Tinc[rows, 0:n_], ALU.subtract)
            k.tt("dve", wv(wst), mpr[rows, 0:n_], mpr[rows, 1:n_ + 1], ALU.subtract)
            last = hi - 1 if fwd else lo
            prev = mnew[rows, last:last + 1]
    Mt3 = Mt[:].rearrange("p (c o) -> p c o", o=1)
    k.tt("dve", i3, i3, Mt3.to_broadcast([36, 24, 64]), ALU.subtract)
    k.act(iT[:], iT[:], AF.Exp)
    k.act(wst[:], wst[:], AF.Exp)
    k.tt("dve", Gg3[:, 0:12, :], Gg3[:, 0:12, :], Mt3[:, 0:12, :].to_broadcast([36, 12, 64]), ALU.subtract)
    k.act(Gg[:, 0:NOWN], Gg[:, 0:NOWN], AF.Exp)
    pl_ = PB[5][:, 0:216].rearrange("p (t r) -> p t r", r=36)
    for t in range(6):
        k.tr(pl_[:, t, :], Gg[0:36, t * 128:(t + 1) * 128], ident_f[0:36, 0:36])
    k.cp("dve", lbTok[:], pl_)
    k.cp("dve", lb8[:, :, 0:4], lbTok[:, :, 0:4]); k.cp("dve", lb8[:, :, 4:8], lbTok[:, :, 32:36])
    pw_ = PB[6][:, 0:96].rearrange("p (a c) -> p a c", c=24)
    for a in range(4):
        k.mm(pw_[:, a, :], cst[0:36, 256 + a * 128: 256 + (a + 1) * 128], wst[0:36, :], True, True)
    k.cp("dve", wbc[:], pw_)
    k.ts("dve", wbc8[:], wbc[:], 0.125, ALU.mult)
    for sq in range(2):
        k.dma("sp", om[sq, 0].rearrange("(h o) -> h o", o=1), mnew[0:4, 4 * sq + 3: 4 * sq + 4])
        k.dma("sp", om[sq, 1].rearrange("(h o) -> h o", o=1), mnew[32:36, 4 * sq: 4 * sq + 1])
    k.tap("eT", iT[:]); k.tap("lbTok", lbTok[:]); k.tap("wbc", wbc[:]); k.tap("mnew", mnew[:]); k.tap("Mt", Mt[:])
    if stage <= 6:
        return k

    S.phase = "D6"
    k.sb_off = offC + 6144
    cmb = k.sb("cmb", [128, 2, 1024], BF16)
    cwt = k.sb("cwt", [128, 4, 9], F32)
    dgs = [k.sb("dg%d" % i, [128, 9, 128], BF16) for i in range(2)]
    Up = k.sb("Up", [128, 2, 258], BF16)
    Us = [k.sb("Us%d" % i, [128, 18, 66], BF16) for i in range(3)]
    kbT = k.sb("kbT", [128, 2, NTOK], BF16, off=xr_off0 + 16384)
    for j_ in range(2):
        k.dma(wq, cmb[:, j_, :], cmask[j_].partition_broadcast(128))
    for f4 in range(4):
        k.dma("sp", cwt[:, f4, :], conv_w[:, :, f4 * 128:(f4 + 1) * 128].rearrange("a b p -> p (a b)"))
    wqb = load_w(w_in, W_QB, 512)
    cm3 = lambda j, R: cmb[:, j, 0:R * 64].rearrange("p (r x) -> p r x", x=64)
    k.memset("pool", Up[:], 0.0)
    for u_ in Us:
        k.memset("pool", u_[:], 0.0)
    for f4 in range(4):
        isq = f4 < 2
        wl = lambda kc: wqb[:, kc, f4 * 128:(f4 + 1) * 128]
        dg = dgs[f4 % 2]
        for tp_ in range(9):
            k.ts("pool", dg[:, tp_, :], ident_f[:], cwt[:, f4, tp_:tp_ + 1], ALU.mult, 1.0, ALU.mult)
        ps = fproj(wl, 128, 0, 512)
        k.cp("act", Up[:, :, 1:257], ps.rearrange("p (s t) -> p s t", s=2))
        if isq:
            ps = fproj(wl, 128, 512, 320)
            k.cp("act", Us[1][:, 1:6, 1:65], ps.rearrange("p (r x) -> p r x", x=64))
            ps = fproj(wl, 128, 1472, 64)
            k.cp("act", Us[1][:, 0:1, 1:65], ps.rearrange("p (r x) -> p r x", x=64))
            R = 4
            blocks = [(0, 4)]
        else:
            for hb in range(2):
                ps = fproj(wl, 128, 512 + hb * 512, 512)
                k.cp("act", Us[1][:, 1 + 8 * hb: 9 + 8 * hb, 1:65], ps.rearrange("p (r x) -> p r x", x=64))
            k.cp("pool", Us[1][:, 0:1, 1:65], Us[1][:, 16:17, 1:65])
            k.cp("pool", Us[1][:, 17:18, 1:65], Us[1][:, 1:2, 1:65])
            R = 16
            blocks = [(0, 8), (8, 8)]
        k.tt("pool", Us[0][:, 0:R, 1:65], Us[1][:, 0:R, 1:65], cm3(0, R), ALU.mult)
        k.tt("pool", Us[2][:, 2:2 + R, 1:65], Us[1][:, 2:2 + R, 1:65], cm3(1, R), ALU.mult)
        ps = nextbank()[:, 0:512].rearrange("p (s t) -> p s t", s=2)
        for b in range(3):
            k.mm(ps, dg[:, 3 + b, :], Up[:, :, b:b + 256], b == 0, b == 2)
        psf = ps.rearrange("p s t -> p (s t)")
        if isq:
            for h2 in range(2):
                rw = slice(64 * h2, 64 * h2 + 64)
                k.act(qbTh[rw, f4, h2, 0:512], psf[rw], AF.Silu)
        else:
            k.act(kbT[:, f4 % 2, 0:512], psf, AF.Silu)
        for (r0, rb) in blocks:
            ps = nextbank()[:, 0:rb * 64].rearrange("p (r x) -> p r x", x=64)
            for tp_ in range(9):
                a, b = tp_ // 3, tp_ % 3
                k.mm(ps, dg[:, tp_, :], Us[a][:, r0 + a:r0 + a + rb, b:b + 64], tp_ == 0, tp_ == 8)
            psf = ps.rearrange("p r x -> p (r x)")
            c0 = 512 + r0 * 64
            if isq:
                for h2 in range(2):
                    rw = slice(64 * h2, 64 * h2 + 64)
                    k.act(qbTh[rw, f4, h2, c0:c0 + rb * 64], psf[rw], AF.Silu)
            else:
                k.act(kbT[:, f4 % 2, c0:c0 + rb * 64], psf, AF.Silu)
    k.tap("kbT", kbT[:])
    S.phase = "D23"
    k.sb_off = offC + 6144
    tseg = k.sb("tseg", [128, 2, 3], F32)
    Lb = k.sb("Lb", [128, NTOK], F32)
    Gs = k.sb("Gs", [128, NTOK], F32)
    Ep = k.sb("Ep", [128, NOWN], F32)
    kraw = k.sb("kraw", [128, NTOK], F32)
    qraw = k.sb("qraw", [128, NOWN], F32)
    Gprev = k.sb("Gprev", [128, 24, 1], F32)
    tot = k.sb("tot", [128, 24, 1], F32)
    for z in range(2):
        k.dma("sp", tseg[:, z, :], tmask[z].rearrange("(s t) -> s t", t=256)[:, 0].partition_broadcast(128))
    wqk = load_w(w_in, 0, 512)
    Lb_b = k.sb("Lb_b", [128, NTOK], F32, off=xr_off0)
    Gs_b = k.sb("Gs_b", [128, NTOK], F32, off=xr_off0 + 6144)
    Ep_b = k.sb("Ep_b", [128, NOWN], F32, off=xr_off0 + 12288)
    Gprev_b = k.sb("Gprev_b", [128, 24, 1], F32, off=xr_off0 + 15360)
    tot_b = k.sb("tot_b", [128, 24, 1], F32, off=xr_off0 + 15488)
    k.memset("pool", Gprev[:, 0:1, :], 0.0)
    k.memset("pool", Gprev_b[:, 0:1, :], 0.0)
    D23SETS = [(Lb, Gs, Ep, Gprev, tot), (Lb_b, Gs_b, Ep_b, Gprev_b, tot_b)]
    for ft in range(2):
        for (t0, n) in TB:
            ps = fproj(lambda kc: wqk[:, kc, 256 + ft * 128: 256 + (ft + 1) * 128], 128, t0, n)
            k.cp("act", kraw[:, t0:t0 + n], ps)
        for (t0, n) in OB:
            ps = fproj(lambda kc: wqk[:, kc, ft * 128:(ft + 1) * 128], 128, t0, n)
            k.act(qraw[:, t0:t0 + n], ps, AF.Copy, scale=0.125)
        for z in range(2):
            Lb, Gs, Ep, Gprev, tot = D23SETS[z]
            L3 = Lb[:].rearrange("p (c j) -> p c j", j=64)
            G3 = Gs[:].rearrange("p (c j) -> p c j", j=64)
            Gend = G3[:, :, 63:64]
            for (t0, n) in TB:
                ps = nextbank()[:, 0:n]
                k.mm(ps, wa2[32 * z:32 * z + 16, ft * 128:(ft + 1) * 128], raT[32 * z:32 * z + 16, t0:t0 + n], True, True)
                k.act(Lb[:, t0:t0 + n], ps, AF.Exp, bias=nba[:, z, ft:ft + 1], scale=-1.0)
            k.act(Lb[:], Lb[:], AF.Ln, bias=1.0)
            for sg in range(3):
                k.ts("dve", Lb[:, 768 + 256 * sg:1024 + 256 * sg], Lb[:, 768 + 256 * sg:1024 + 256 * sg], tseg[:, z, sg:sg + 1], ALU.mult)
            k.scan(Gs[:], ones_b(128, NTOK), Lb[:], 0.0, ALU.mult, ALU.add)
            k.cp("pool", Gprev[:, 1:24, :], Gend[:, 0:23, :])
            k.tt("dve", tot[:], Gend, Gprev[:], ALU.subtract)
            k.act(ebl[:, z, ft, :], tot[:, :, 0], AF.Exp, scale=-1.0 / 16)
            if z == 0:
                k.tt("dve", G3, G3, Gprev[:].to_broadcast([128, 24, 64]), ALU.subtract)
                Pm = Gs
            else:
                k.tt("dve", L3, L3, G3, ALU.subtract)
                k.tt("dve", L3, L3, Gend.to_broadcast([128, 24, 64]), ALU.add)
                Pm = Lb
            k.act(Ep[:], Pm[:, 0:NOWN], AF.Exp, scale=-1.0 / 16)
            k.act(Pm[:], Pm[:], AF.Exp, scale=1.0 / 16)
            k.tt("pool", kT[:, z, ft, :], kraw[:], Pm[:], ALU.mult)
            k.tt("pool", qTh[0:64, z, ft, 0, :], qraw[0:64, :], Ep[0:64, :], ALU.mult)
            k.tt("pool", qTh[64:128, z, ft, 1, :], qraw[64:128, :], Ep[64:128, :], ALU.mult)
    k.tap("kT", kT[:]); k.tap("ebl", ebl[:])
    if stage <= 4:
        return k

    S.phase = "D7"
    for z in range(2):
        for pr in range(2):
            a = 2 * z + pr
            for (t0, n) in TB:
                ps = nextbank()[:, 0:n]
                k.mm(ps, cst[0:36, 256 + a * 128: 256 + (a + 1) * 128], iT[0:36, t0:t0 + n], True, True)
                k.tt("dve", kbTe[:, z, pr, t0:t0 + n], kbT[:, pr, t0:t0 + n], ps, ALU.mult)
    k.tap("kbTe", kbTe[:])
    if stage <= 7:
        return k

    S.phase = "D4"
    k.sb_off = offD
    gtmp = [k.sb("gtmp%d" % i, [128, 512], F32) for i in range(2)]
    d4bank = [0]
    for (c0, kind) in ((W_VA, "va"), (W_VB, "vb"), (W_GA, "ga"), (W_OB, "ob")):
        wb = d4_w[kind] if kind in d4_w else load_w(w_in, c0, 512, buf=2 if kind == "ga" else 3)
        for t in range(12 if kind in ("va", "vb") else 6):
            ps = PB[4 + d4bank[0] % 4][:, :]; d4bank[0] += 1
            for kc in range(8):
                k.mm(ps, hT[:, kc, t * 128:(t + 1) * 128], wb[:, kc, :], kc == 0, kc == 7)
            if kind == "va":
                k.cp("act", v_tok[:, t, :], ps)
            elif kind == "vb":
                k.cp("act", vb_aug[:, t, :, 0:128], ps.rearrange("p (h v) -> p h v", h=4))
            elif kind == "ga":
                k.act(gtmp[t % 2][:], ps, AF.Silu)
                k.tt("pool", Ga[:, t, :], gtmp[t % 2][:], gnwa[:], ALU.mult)
            else:
                k.act(gtmp[t % 2][:], ps, AF.Sigmoid)
                k.tt("pool", Gb[:, t, :], gtmp[t % 2][:], gnwb[:], ALU.mult)
    k.tap("v_tok", v_tok[:]); k.tap("vb_aug", vb_aug[:]); k.tap("Ga", Ga[:]); k.tap("Gb", Gb[:])
    if stage <= 5:
        return k

    wo = k.sb("wo", [128, 8, 1024], BF16, off=offA)
    for hh in range(2):
        k.dma(wq, wo[:, :, hh * 512:(hh + 1) * 512], w_out[:, hh * 512:(hh + 1) * 512].rearrange("(kc p) n -> p kc n", p=128))
    S.phase = "E"
    k.sb_off = offB
    ktGM = [k.sb("ktGM%d" % i, [128, 2, 2, 4, 128], BF16) for i in range(2)]
    Sst0 = k.sb("Sst", [128, 2, 2, 128], F32)
    Stmp0 = k.sb("Stmp", [128, 2, 2, 128], F32)
    CNst0 = k.sb("CNst", [128, 2, 2, 130], F32)
    Ssn0 = k.sb("Ssn", [128, 2, 2, 4, 128], BF16)
    CNsn0 = k.sb("CNsn", [128, 2, 2, 4, 130], BF16)
    xo = k.base_of[xres.name][1]
    Sst1 = k.sb("Sst1", [128, 2, 2, 128], F32, off=xo)
    Stmp1 = k.sb("Stmp1", [128, 2, 2, 128], F32, off=xo + 2048)
    CNst1 = k.sb("CNst1", [128, 2, 2, 130], F32, off=xo + 4096)
    Ssn1 = k.sb("Ssn1", [128, 2, 2, 4, 128], BF16, off=xo + 6272)
    CNsn1 = k.sb("CNsn1", [128, 2, 2, 4, 130], BF16, off=xo + 10368)
    ESETS = [(Sst0, Stmp0, CNst0, Ssn0, CNsn0), (Sst1, Stmp1, CNst1, Ssn1, CNsn1)]
    scbG = [k.sb("scbG%d" % i, [128, 2, 2, 4, 64], BF16) for i in range(2)]
    scbM = [k.sb("scbM%d" % i, [128, 2, 2, 4, 64], BF16) for i in range(2)]
    for i in range(2):
        k.memset("pool", ktGM[i][:], 0.0)
        k.memset("pool", scbG[i][:], 0.0); k.memset("pool", scbM[i][:], 0.0)
    mask2 = k.sb("mask2", [128, 2, 64], F32)
    mask2s = k.sb("mask2s", [128, 2, 64], F32)
    hstat = k.sb("hstat", [128, 16], F32)
    dm8 = k.sb("dm8", [128, 8], F32)
    hsum = k.sb("hsum", [128, 512], F32)
    mixb = [k.sb("mixb%d" % i, [128, 1024], BF16) for i in range(2)]
    mixT_h = k.sb("mixT", [128, 8, NOWN], BF16, off=hT_off)
    mixT = mixT_h[:]
    for par in range(2):
        k.cp("pool", mask2[64 * par:64 * par + 64, 0, :], maskU[64 * par:64 * par + 64, 64 * par:64 * par + 64])
        k.cp("pool", mask2[64 * par:64 * par + 64, 1, :], maskL[64 * par:64 * par + 64, 64 * par:64 * par + 64])
    k.ts("dve", mask2s[:], mask2[:], 0.125, ALU.mult)
    stateview = lambda d: d.rearrange("(pr h2) d v -> (h2 d) pr v", pr=2)

    def seq_scan(sq):
        Sst, Stmp, CNst, Ssn, CNsn = ESETS[sq % 2]
        sample = sq == 2
        own_tiles = [2 * sq, 2 * sq + 1]
        c0 = 4 * sq
        if sample:
            orders = [list(range(12, 24)) + list(range(8, 12)), list(range(23, 11, -1)) + list(range(11, 7, -1))]
            tiles = list(range(6, 12)) + own_tiles
        else:
            orders = [list(range(c0, c0 + 4)), list(range(c0 + 3, c0 - 1, -1))]
            tiles = own_tiles
        S.phase = "Ec%d" % sq
        if sample:
            for z in range(2):
                k.dma("sp", Sst[:, z, :, :], stateview(sgla[z]))
                k.dma("sp", CNst[:, z, :, 0:128], stateview(sC[z]))
                k.dma("sp", CNst[:, z, :, 128], sn[z].rearrange("(pr h2) d -> (h2 d) pr", pr=2))
            k.memset("pool", CNst[:, :, :, 129:130], 0.0)
        else:
            k.memset("pool", Sst[:], 0.0)
            k.memset("pool", CNst[:], 0.0)
        ktile = [None, None]
        kcnt = [0, 0]

        def get_kt(t, z):
            if ktile[z] is not None and ktile[z][0] == t:
                return ktile[z][1], ktile[z][2]
            i = kcnt[z] % 2; kcnt[z] += 1
            pt = pbf(4).rearrange("p (a b) -> p a b", b=128)
            for ft in range(2):
                k.tr(pt[:, 4 * z + ft, :], kT[:, z, ft, t * 128:(t + 1) * 128], ident_b[:])
                k.tr(pt[:, 4 * z + 2 + ft, :], kbTe[:, z, ft, t * 128:(t + 1) * 128], ident_b[:])
            for par in range(2):
                rw = slice(64 * par, 64 * par + 64)
                if t >= 6:
                    k.act(ktGM[i][rw, par, z, :, :], pt[rw, 4 * z:4 * z + 4, :], AF.Copy, scale=tmk[rw, t - 6, z:z + 1])
                else:
                    k.cp("act", ktGM[i][rw, par, z, :, :], pt[rw, 4 * z:4 * z + 4, :])
            ktile[z] = (t, ktGM[i], ktGM[i])
            return ktGM[i], ktGM[i]

        pend = [None, None]
        for step in range(len(orders[0])):
            for z in range(2):
                c = orders[z][step]
                t = c // 2; par = c % 2
                kg, km = get_kt(t, z)
                own = (c // 4) == (sq if not sample else 2)
                cl = c - c0 if not sample else c - 8
                rows = slice(64 * par, 64 * par + 64)
                psS = PB[0][:, 256 * z:256 * z + 256].rearrange("p (f v) -> p f v", f=2)
                for ft in range(2):
                    for h2 in range(2):
                        k.mm(psS[64 * h2:64 * h2 + 64, ft, :], kg[:, par, z, ft, 64 * h2:64 * h2 + 64],
                             v_tok[:, t, (2 * ft + h2) * 128:(2 * ft + h2 + 1) * 128], True, True)
                for ft in range(2):
                    sc_prev = ones1[:, 0:1] if pend[z] is None else ebl[:, z, ft, pend[z]:pend[z] + 1]
                    if own:
                        k.act(Ssn[:, z, ft, cl, :], Sst[:, z, ft, :], AF.Copy, scale=sc_prev)
                    k.stt(Sst[:, z, ft, :], Sst[:, z, ft, :], sc_prev, psS[:, ft, :], ALU.mult, ALU.add)
                pend[z] = c
                psC = PB[2 + z][:, 0:260].rearrange("p (f v) -> p f v", f=2)
                for pr in range(2):
                    for h2 in range(2):
                        k.mm(psC[64 * h2:64 * h2 + 64, pr, 0:129], km[:, par, z, 2 + pr, 64 * h2:64 * h2 + 64],
                             vb_aug[:, t, 2 * pr + h2, 0:129], True, True)
                for pr in range(2):
                    if own:
                        k.act(CNsn[:, z, pr, cl, :], CNst[:, z, pr, :], AF.Copy, scale=wbc8[:, 2 * z + pr, c:c + 1])
                    k.stt(CNst[:, z, pr, 0:129], CNst[:, z, pr, 0:129], wbc[:, 2 * z + pr, c:c + 1], psC[:, pr, 0:129], ALU.mult, ALU.add)
        if not sample:
            for z in range(2):
                k.tt("dve", Sst[:, z], Sst[:, z], ebl[:, z, :, pend[z]:pend[z] + 1].to_broadcast([128, 2, 128]), ALU.mult)
                k.dma("sp", stateview(oS[sq, z]), Sst[:, z, :, :])
                k.dma("sp", stateview(oC[sq, z]), CNst[:, z, :, 0:128])
                k.dma("sp", on[sq, z].rearrange("(pr h2) d -> (h2 d) pr", pr=2), CNst[:, z, :, 128])
        S.phase = "Eo%d" % sq
        for ti, t in enumerate(own_tiles):
            i = ti % 2
            tok0 = t * 128
            scG = PB[5][:, :].rearrange("p (z h x) -> p z h x", z=2, h=4)
            scM = PB[6][:, :].rearrange("p (z h x) -> p z h x", z=2, h=4)
            for z in range(2):
                for h in range(4):
                    fr = slice(64 * (h % 2), 64 * (h % 2) + 64)
                    for par in range(2):
                        tk = slice(tok0 + 64 * par, tok0 + 64 * par + 64)
                        rows = slice(64 * par, 64 * par + 64)
                        k.mm(scG[rows, z, h, :], kT[:, z, h // 2, tk], qTh[:, z, h // 2, h % 2, tk], True, True)
                        k.mm(scM[rows, z, h, :], kbTe[:, z, h // 2, tk], qbTh[:, h // 2, h % 2, tk], True, True)
            for z in range(2):
                for par in range(2):
                    rw = slice(64 * par, 64 * par + 64)
                    k.tt("dve", scbG[i][rw, par, z], scG[rw, z], mask2[rw, z:z + 1, :].to_broadcast([64, 4, 64]), ALU.mult)
                    k.tt("dve", scbM[i][rw, par, z], scM[rw, z], mask2s[rw, z:z + 1, :].to_broadcast([64, 4, 64]), ALU.mult)
            oG = PB[7][:, :]

            def ndslot(s_):
                return PB[(0, 2, 3)[s_ // 3]][:, (s_ % 3) * 129:(s_ % 3) * 129 + 129]
            for par in range(2):
                rows = slice(64 * par, 64 * par + 64)
                tk = slice(tok0 + 64 * par, tok0 + 64 * par + 64)
                cl = 2 * ti + par
                for h in range(4):
                    fr = slice(64 * (h % 2), 64 * (h % 2) + 64)
                    og = oG[rows, 128 * h:128 * h + 128]
                    k.mm(og, scbG[i][:, par, 0, h, :], v_tok[:, t, 128 * h:128 * h + 128], True, False)
                    k.mm(og, qTh[:, 0, h // 2, h % 2, tk], Ssn[:, 0, h // 2, cl, :], False, False)
                    k.mm(og, scbG[i][:, par, 1, h, :], v_tok[:, t, 128 * h:128 * h + 128], False, False)
                    k.mm(og, qTh[:, 1, h // 2, h % 2, tk], Ssn[:, 1, h // 2, cl, :], False, True)
                    for z in range(2):
                        nd = ndslot(4 * z + h)[rows, :]
                        k.mm(nd, scbM[i][:, par, z, h, :], vb_aug[:, t, h, 0:129], True, False)
                        k.mm(nd, qbTh[:, h // 2, h % 2, tk], CNsn[:, z, h // 2, cl, 0:129], False, True)
            mx = mixb[i]
            for h in range(4):
                k.act(hsum[:, 128 * h:128 * h + 128], oG[:, 128 * h:128 * h + 128], AF.Square, accum=hstat[:, h:h + 1])
            k.act(hstat[:, 0:4], hstat[:, 0:4], AF.Sqrt, bias=eps_t[:, 0:1], scale=1.0 / 128)
            k.recip(hstat[:, 0:4], hstat[:, 0:4])
            for h in range(4):
                k.stt(mx[:, 128 * h:128 * h + 128], oG[:, 128 * h:128 * h + 128], hstat[:, h:h + 1],
                      Ga[:, t, 128 * h:128 * h + 128], ALU.mult, ALU.mult)
            for bk, (s0, ns) in enumerate(((0, 3), (3, 3), (6, 2))):
                den = PB[(0, 2, 3)[bk]][:, 0:ns * 129].rearrange("p (s v) -> p s v", v=129)[:, :, 128]
                k.act(dm8[:, s0:s0 + ns], den, AF.Abs)
            k.tt("dve", dm8[:], dm8[:], lb8[:, t, :], ALU.max)
            k.recip(dm8[:], dm8[:])
            for h in range(4):
                k.act(hsum[:, 128 * h:128 * h + 128], ndslot(h)[:, 0:128], AF.Copy, scale=dm8[:, h:h + 1])
            for h in range(4):
                k.stt(hsum[:, 128 * h:128 * h + 128], ndslot(4 + h)[:, 0:128], dm8[:, 4 + h:5 + h],
                      hsum[:, 128 * h:128 * h + 128], ALU.mult, ALU.add)
            for h in range(4):
                k.act(mx[:, 512 + 128 * h:512 + 128 * h + 128], hsum[:, 128 * h:128 * h + 128], AF.Square, accum=hstat[:, 4 + h:5 + h])
            k.act(hstat[:, 4:8], hstat[:, 4:8], AF.Sqrt, bias=eps_t[:, 0:1], scale=1.0 / 128)
            k.recip(hstat[:, 4:8], hstat[:, 4:8])
            for h in range(4):
                k.stt(mx[:, 512 + 128 * h:512 + 128 * h + 128], hsum[:, 128 * h:128 * h + 128], hstat[:, 4 + h:5 + h],
                      Gb[:, t, 128 * h:128 * h + 128], ALU.mult, ALU.mult)
            pta = pbf(4).rearrange("p (a b) -> p a b", b=128)
            for kc in range(8):
                k.tr(pta[:, kc, :], mx[:, kc * 128:(kc + 1) * 128], ident_b[:])
            k.cp("act", mixT[:, :, tok0:tok0 + 128], pta[:, :, :])

    for sq in range(3):
        seq_scan(sq)
    k.tap("mixT", mixT)
    if stage <= 8:
        return k

    S.phase = "Blate"
    soffL = hT_off + 8 * NOWN * 2
    g2_off = k.base_of[gb_row.name][1] + 8192
    xr_off = k.base_of[xres.name][1]
    wadaL = [k.sb("wadaL0", [128, 8, 512], BF16, off=soffL), k.sb("wadaL1", [128, 8, 512], BF16, off=g2_off)]
    mrL = k.sb("mrL", [2, 1024], F32, off=soffL + 8192)
    baddL = k.sb("baddL", [2, 512], F32, off=229376 - 2112)
    for piece in (2, 3, 4):
        mod_piece(piece, wadaL, mrL, baddL, hw=512, banks=(1, 1, 1, 1), pin=[mixT[:, 0, 128:256]])
    S.phase = "cast"
    sc1 = nc.dram_tensor("sc_ff1", [8, 128, 4096], BF16, kind="Internal").ap()
    sc2 = nc.dram_tensor("sc_ff2", [8, 128, 4096], BF16, kind="Internal").ap()
    S.tracked_dram.add(sc1.tensor.name); S.tracked_dram.add(sc2.tensor.name)
    for fb in range(8):
        k.dma(wq, sc1[fb].rearrange("p (kc n) -> p kc n", kc=8), w_ff1[:, fb * 512:(fb + 1) * 512].rearrange("(kc p) n -> p kc n", p=128))
        k.dma(wq, sc2[fb].rearrange("p (ft n) -> p ft n", ft=4), w_ff2[512 * fb:512 * (fb + 1), :].rearrange("(ft p) n -> p ft n", p=128))
    S.phase = "F"
    soff = hT_off + 8 * NOWN * 2
    ytmp = k.sb("ytmp", [128, 1024], F32, off=soff)
    junk2 = k.sb("junk2", [128, 1024], BF16, off=soff + 4096)
    xn2 = [k.sb("xn2_%d" % i, [128, 1024], BF16, off=soff + 6144 + 2048 * i) for i in range(2)]
    h2T = mixT_h
    junk_sv, xn_sv = junk, xn

    def norm_to_T2(xt, si, dstT, tcol, c, a_sh, a_sc, par):
        k.act(junk2[:], xt, AF.Square, accum=ssq[:, si:si + 1])
        k.act(rstd[:, si:si + 1], ssq[:, si:si + 1], AF.Sqrt, bias=eps_t[:, 0:1], scale=1.0 / 1024)
        k.recip(rstd[:, si:si + 1], rstd[:, si:si + 1])
        xnb = xn2[par]
        k.act(xnb[:], xt, AF.Copy, scale=rstd[:, si:si + 1])
        pta = pbf(4 + 2 * par).rearrange("p (a b) -> p a b", b=128)
        ptb = pbf(5 + 2 * par).rearrange("p (a b) -> p a b", b=128)
        for kc in range(8):
            k.tr((pta if kc < 4 else ptb)[:, kc % 4, :], xnb[:, kc * 128:(kc + 1) * 128], ident_b[:])
        for kc in range(8):
            if kc < 4:
                k.ts("dve", dstT[:, kc, tcol:tcol + 128], pta[:, kc, :], modT[:, a_sc, kc, c:c + 1], ALU.mult,
                     modT[:, a_sh, kc, c:c + 1], ALU.add)
            else:
                k.act(dstT[:, kc, tcol:tcol + 128], ptb[:, kc - 4, :], AF.Identity, bias=modT[:, a_sh, kc, c:c + 1],
                      scale=modT[:, a_sc, kc, c:c + 1])

    for t in range(6):
        k.dma("sp", xres[:, t, :], xin[t * 128:(t + 1) * 128, :])
    for t in range(6):
        c = 0 if t < 4 else 1
        for hh in range(2):
            ps = PB[hh][:, :]
            for kc in range(8):
                k.mm(ps, mixT[:, kc, t * 128:(t + 1) * 128], wo[:, kc, hh * 512:(hh + 1) * 512], kc == 0, kc == 7)
            k.tt("dve", ytmp[:, hh * 512:(hh + 1) * 512], ps, gb_row[:, 0, c, hh * 512:(hh + 1) * 512], ALU.mult)
        k.tt("pool", xres[:, t, :], xres[:, t, :], ytmp[:], ALU.add)
        norm_to_T2(xres[:, t, :], t, h2T, t * 128, c, 2, 3, t % 2)
    k.tap("h2T", h2T[:])
    if stage <= 9:
        return k

    S.phase = "G"
    uT = k.sb("uT", [128, 32, NOWN], BF16, off=offA)
    ff1s = [k.sb("ff1s%d" % i, [128, 8, 512], BF16, off=offA + 49152 + 8192 * i) for i in range(2)]
    wf2 = k.sb("wf2", [128, 32, 1024], BF16, off=offA + 65536)
    fnw = k.sb("fnw", [128, 1024], F32, off=soff)
    ytmp2 = k.sb("ytmp2", [128, 1024], F32, off=soff + 4096)
    rtmp = [k.sb("rtmp%d" % i, [128, 512], F32, off=soff + 8192 + 2048 * i) for i in range(2)]
    g1_off = k.base_of[gb_row.name][1]
    outst = [k.sb("outst%d" % i, [128, 1024], F32, off=g1_off + 4096 * i) for i in range(2)]
    k.dma("sp", fnw[:], final_norm_w.partition_broadcast(128))
    rc = 0
    for fb in range(8):
        wb = ff1s[fb % 2]
        k.dma("sp", wb[:], sc1[fb].rearrange("p (kc n) -> p kc n", kc=8))
        k.dma("sp", wf2[:, 4 * fb:4 * fb + 4, :], sc2[fb].rearrange("p (ft n) -> p ft n", ft=4))
        for fi in range(4):
            ftile = fb * 4 + fi
            for (t0, n) in OB:
                ps = nextbank()[:, 0:n]
                for kc in range(8):
                    k.mm(ps, wb[:, kc, fi * 128:(fi + 1) * 128], h2T[:, kc, t0:t0 + n], kc == 0, kc == 7)
                rt = rtmp[rc % 2]; rc += 1
                k.act(rt[:, 0:n], ps, AF.Relu)
                k.tt("dve", uT[:, ftile, t0:t0 + n], rt[:, 0:n], rt[:, 0:n], ALU.mult)
    wadaG = [k.sb("wadaG%d" % i, [128, 8, 256], BF16, off=g1_off + 4096 * i) for i in range(2)]
    mrG = k.sb("mrG", [2, 1024], F32, off=soff + 4096)
    mod_piece(5, wadaG, mrG, baddL, hw=256, banks=(4, 4, 4, 4))
    for t in range(6):
        c = 0 if t < 4 else 1
        for hh in range(2):
            ps = PB[4 + 2 * (t % 2) + hh][:, :]
            for ft in range(32):
                k.mm(ps, uT[:, ft, t * 128:(t + 1) * 128], wf2[:, ft, hh * 512:(hh + 1) * 512], ft == 0, ft == 31)
            k.tt("dve", ytmp2[:, hh * 512:(hh + 1) * 512], ps, gb_row[:, 1, c, hh * 512:(hh + 1) * 512], ALU.mult)
        k.tt("dve", xres[:, t, :], xres[:, t, :], ytmp2[:], ALU.add)
        k.act(junk2[:], xres[:, t, :], AF.Square, accum=ssq[:, 8 + t:9 + t])
        k.act(rstd[:, 8 + t:9 + t], ssq[:, 8 + t:9 + t], AF.Sqrt, bias=eps_t[:, 0:1], scale=1.0 / 1024)
        k.recip(rstd[:, 8 + t:9 + t], rstd[:, 8 + t:9 + t])
        ob_ = outst[t % 2]
        k.stt(ob_[:], xres[:, t, :], rstd[:, 8 + t:9 + t], fnw[:], ALU.mult, ALU.mult)
        k.dma("sp", yo[t * 128:(t + 1) * 128, :], ob_[:])
    return k


def finish(k):
    with k.nc.allow_non_contiguous_dma(reason="param loads"):
        block = k.st.enter_context(k.nc.Block())
        k.S.emit(block)
    k.st.close()
    return k.nc


def make_consts():
    c = np.zeros((128, 1024), np.float32)
    p = np.arange(128)[:, None]; j = np.arange(128)[None, :]
    c[:, 0:128] = (p <= j)
    c[:, 128:256] = (p >= j)
    for d in range(2):
        for pr in range(2):
            blk = np.zeros((128, 128), np.float32)
            for pp in range(128):
                blk[32 * d + 2 * pr + (pp >= 64), pp] = 1.0
            c[:, 256 + (2 * d + pr) * 128: 256 + (2 * d + pr + 1) * 128] = blk
    return c


def make_in_maps(inp):
    f = lambda a: np.ascontiguousarray(np.asarray(a, dtype=np.float32))
    consts = make_consts()
    maps = []
    for core in range(NCORES):
        b, q = core // 4, core % 4
        xs = np.roll(inp["x_sample"][b], -256 * q, axis=0)
        xin = np.concatenate([inp["x_prompt"][2 * core], inp["x_prompt"][2 * core + 1], xs], axis=0)
        mf = np.array([1.0 if (q + j) >= 4 else 0.0 for j in range(1, 4)], np.float32)
        mb = np.array([1.0 if (q + j) < 4 else 0.0 for j in range(1, 4)], np.float32)
        tmask = np.stack([np.repeat(mf, 256), np.repeat(mb, 256)])
        true_row = (4 * q + np.arange(16)) % 16
        mu = (true_row != 0).astype(np.float32)
        md = (true_row != 15).astype(np.float32)
        cmask = np.stack([np.repeat(mu, 64), np.repeat(md, 64)])
        m = {
            "xin": f(xin), "cond": f(np.stack([inp["c_ctx"], inp["c"][b]])),
            "sgla": f(inp["state_gla"][b, 0]), "sC": f(inp["state_mlstm_C"][b, 0]),
            "sn": f(inp["state_mlstm_n"][b, 0]), "sm": f(inp["state_mlstm_m"][b, 0]),
            "tmask": f(tmask), "tmaskT": f(tmask.reshape(2, 6, 128).transpose(2, 1, 0)), "cmask": f(cmask), "consts": consts,
            "w_ada": f(inp["w_ada"][0]), "b_ada": f(inp["b_ada"][0]),
            "norm1_w": f(inp["norm1_w"][0]), "norm2_w": f(inp["norm2_w"][0]),
            "w_in": f(inp["w_in"][0]), "w_alpha2": f(inp["w_alpha2"][0]), "b_alpha": f(inp["b_alpha"][0]),
            "b_mgate": f(inp["b_mgate"][0]), "conv_w": f(inp["conv_w"][0]),
            "gnorm_a_w": f(inp["gnorm_a_w"][0]), "gnorm_b_w": f(inp["gnorm_b_w"][0]),
            "w_out": f(inp["w_out"][0]), "w_ff1": f(inp["w_ff1"][0]), "w_ff2": f(inp["w_ff2"][0]),
            "final_norm_w": f(inp["final_norm_w"]),
        }
        maps.append(m)
    return maps


_NC_CACHE = {}


def kernel(**inp):
    if "nc" not in _NC_CACHE:
        _NC_CACHE["nc"] = finish(build())
    nc = _NC_CACHE["nc"]
    maps = make_in_maps(inp)
    res = run_bass_kernel_spmd(nc, maps, core_ids=list(range(NCORES))).results
    y_prompt = np.zeros((16, 256, 1024), np.float32)
    y_sample = np.zeros((2, 1024, 1024), np.float32)
    nS = np.zeros((16, 1, 2, 4, 64, 128), np.float32); nC = np.zeros((16, 1, 2, 4, 64, 128), np.float32)
    nn = np.zeros((16, 1, 2, 4, 64), np.float32); nm = np.zeros((16, 1, 2, 4), np.float32)
    for core in range(NCORES):
        r = res[core]
        b, q = core // 4, core % 4
        y = np.asarray(r["yo"], np.float32)
        y_prompt[2 * core] = y[0:256]; y_prompt[2 * core + 1] = y[256:512]
        y_sample[b, 256 * q:256 * q + 256] = y[512:768]
        nS[2 * core:2 * core + 2, 0] = np.asarray(r["oS"]); nC[2 * core:2 * core + 2, 0] = np.asarray(r["oC"])
        nn[2 * core:2 * core + 2, 0] = np.asarray(r["on"]); nm[2 * core:2 * core + 2, 0] = np.asarray(r["om"])
    return (y_prompt, y_sample, nS, nC, nn, nm)
```

```python
import numpy as np
from contextlib import ExitStack
import concourse.bass as bass
import concourse.mybir as mybir
from concourse.bass_utils import run_bass_kernel_spmd

F32 = mybir.dt.float32
BF16 = mybir.dt.bfloat16
AF = mybir.ActivationFunctionType
ALU = mybir.AluOpType
AX = mybir.AxisListType
ENGS = ("pe", "act", "dve", "pool", "sp")
NCORES = 8
BIG = 1.0e30


def _dsize(dt):
    return mybir.dt.size(dt)


class Sched:
    def __init__(self, nc, esems, dsems, base_of):
        self.nc = nc
        self.esem = esems
        self.dsems = dsems
        nq = len(dsems) // 2
        self.dpool = {"sp": list(range(0, nq)), "pool": list(range(nq, len(dsems))), "act": list(range(0, nq))}
        self.ops = []
        self.rec = {}
        self.base_of = base_of
        self.same_engine_sync = True
        self.phase = "init"
        self.reorder = True
        self.tracked_dram = set()

    def _box(self, a):
        if str(a.space) == "DRAM":
            off = int(a.offset); sz = _dsize(a.dtype)
            lo = off; hi = off
            for st, cn in a.ap:
                d = st * (cn - 1)
                if d < 0:
                    lo += d
                else:
                    hi += d
            return ("D_" + a.tensor.name, 0, 1, lo * sz, (hi + 1) * sz, ((lo * sz, (hi + 1) * sz),))
        ap = a.ap
        pstep, pcnt = ap[0]
        off = int(a.offset)
        sz = _dsize(a.dtype)
        if pstep > 0:
            p0 = off // pstep
            fo = off % pstep
        else:
            p0 = 0
            fo = off
        lo = fo
        hi = fo
        for st, cn in ap[1:]:
            d = st * (cn - 1)
            if d < 0:
                lo += d
            else:
                hi += d
        key, base = self.base_of[a.tensor.name]
        if key[0] == "P":
            return (key, (p0 // 32) * 32, ((p0 + pcnt + 31) // 32) * 32, 0, 1 << 30, ((0, 1 << 30),))
        dims = [(st, cn) for st, cn in ap[1:] if cn > 1]
        rows = None
        if len(dims) >= 2 and all(st > 0 for st, cn in dims):
            ist, icn = dims[-1]
            ilen = (icn - 1) * ist + 1
            nouter = 1
            for st, cn in dims[:-1]:
                nouter *= cn
            if nouter <= 1024:
                offs = [0]
                for st, cn in dims[:-1]:
                    offs = [o + st * i for o in offs for i in range(cn)]
                iv = sorted((base + (fo + o) * sz, base + (fo + o + ilen) * sz) for o in offs)
                merged = [list(iv[0])]
                for s_, e_ in iv[1:]:
                    if s_ <= merged[-1][1]:
                        if e_ > merged[-1][1]:
                            merged[-1][1] = e_
                    else:
                        merged.append([s_, e_])
                rows = tuple((x[0], x[1]) for x in merged)
        if rows is None:
            rows = ((base + lo * sz, base + (hi + 1) * sz),)
        return (key, p0, p0 + pcnt, base + lo * sz, base + (hi + 1) * sz, rows)

    @staticmethod
    def _ovl(b1, b2):
        if not (b1[1] < b2[2] and b2[1] < b1[2] and b1[3] < b2[4] and b2[3] < b1[4]):
            return False
        r1, r2 = b1[5], b2[5]
        if len(r1) == 1 and len(r2) == 1:
            return True
        i = j = 0
        while i < len(r1) and j < len(r2):
            if r1[i][0] < r2[j][1] and r2[j][0] < r1[i][1]:
                return True
            if r1[i][1] <= r2[j][1]:
                i += 1
            else:
                j += 1
        return False

    @staticmethod
    def _contains(b1, b2):
        if not (b1[1] <= b2[1] and b2[2] <= b1[2] and b1[3] <= b2[3] and b2[4] <= b1[4]):
            return False
        r1, r2 = b1[5], b2[5]
        i = 0
        for s_, e_ in r2:
            while i < len(r1) and r1[i][1] < e_:
                i += 1
            if i >= len(r1) or not (r1[i][0] <= s_ and e_ <= r1[i][1]):
                return False
        return True

    def _deps(self, reads, writes):
        deps = set()
        trk = lambda a: str(a.space) != "DRAM" or a.tensor.name in self.tracked_dram
        rb = [self._box(a) for a in reads if trk(a)]
        wb = [self._box(a) for a in writes if trk(a)]
        for b in rb:
            ps = b[0][0] == "P"
            for r in self.rec.get(b[0], ()):
                if (r[1] == "w" or ps) and self._ovl(b, r[0]):
                    deps.add(r[2])
        for b in wb:
            for r in self.rec.get(b[0], ()):
                if self._ovl(b, r[0]):
                    deps.add(r[2])
        return deps, rb, wb

    def _commit(self, rb, wb, oid):
        for b in wb:
            lst = self.rec.setdefault(b[0], [])
            lst[:] = [r for r in lst if not self._contains(b, r[0])]
            lst.append([b, "w", oid])
        for b in rb:
            lst = self.rec.setdefault(b[0], [])
            lst.append([b, "r", oid])

    def op(self, eng, fn, reads=(), writes=(), dur=0.2):
        deps, rb, wb = self._deps(reads, writes)
        oid = len(self.ops)
        self.ops.append(dict(eng=eng, kind="c", fn=fn, deps=deps, dur=dur, phase=self.phase))
        self._commit(rb, wb, oid)

    def dma(self, q, out, in_, **kw):
        deps, rb, wb = self._deps([in_], [out])
        oid = len(self.ops)
        nbytes = 1
        for d in out.shape:
            nbytes *= d
        nbytes *= max(_dsize(out.dtype), _dsize(in_.dtype))

        def fn(e, out=out, in_=in_, kw=kw):
            return e.dma_start(out=out, in_=in_, **kw)

        self.ops.append(dict(eng=q, kind="d", fn=fn, deps=deps, dur=0.1, bytes=nbytes, phase=self.phase,
                             isout=str(out.space) == "DRAM"))
        self._commit(rb, wb, oid)

    def _schedule(self):
        ops = self.ops
        n = len(ops)
        succ = [[] for _ in range(n)]
        for i, o in enumerate(ops):
            for d in o["deps"]:
                succ[d].append(i)
        lat = [(o["dur"] if o["kind"] == "c" else 3.0 + o["bytes"] / 200e3) for o in ops]
        rank = [0.0] * n
        for i in range(n - 1, -1, -1):
            r = 0.0
            for j in succ[i]:
                if rank[j] > r:
                    r = rank[j]
            rank[i] = r + lat[i]
        if not self.reorder:
            return {e: [i for i, o in enumerate(ops) if o["eng"] == e] for e in ENGS}, 0.0
        import heapq
        ndep = [len(o["deps"]) for o in ops]
        fin = [None] * n
        ready_t = [0.0] * n
        cand = {e: [] for e in ENGS}
        fifo = {e: [i for i, o in enumerate(ops) if o["eng"] == e and o["kind"] == "d"] for e in ENGS}
        fpos = {e: 0 for e in ENGS}
        for i, o in enumerate(ops):
            if ndep[i] == 0:
                cand[o["eng"]].append(i)
        efree = {e: 0.0 for e in ENGS}
        dma_free = {e: 0.0 for e in ENGS}
        order = {e: [] for e in ENGS}
        done = 0
        while done < n:
            best = None
            for e in ENGS:
                head = fifo[e][fpos[e]] if fpos[e] < len(fifo[e]) else -1
                for i in cand[e]:
                    if ops[i]["kind"] == "d" and i != head:
                        continue
                    st = max(efree[e], ready_t[i])
                    key = (st, -rank[i], i)
                    if best is None or key < best[0]:
                        best = (key, e, i)
            assert best is not None, "scheduler stuck"
            (st, _, _), e, i = best
            o = ops[i]
            o["st"] = st
            cand[e].remove(i)
            if o["kind"] == "d":
                fpos[e] += 1
                efree[e] = st + o["dur"]
                dstart = max(st + 1.0, dma_free[e])
                dma_free[e] = dstart + o["bytes"] / 230e3
                fin[i] = dma_free[e] + 1.5
            else:
                efree[e] = st + o["dur"]
                fin[i] = efree[e]
            order[e].append(i)
            done += 1
            for j in succ[i]:
                ndep[j] -= 1
                if fin[i] > ready_t[j]:
                    ready_t[j] = fin[i]
                if ndep[j] == 0:
                    cand[ops[j]["eng"]].append(j)
        return order, max(f for f in fin if f is not None)

    def emit(self, block):
        ops = self.ops
        order, est = self._schedule()
        self.est_makespan = est
        self.order = order
        self.labels = {e: [ops[i]["phase"] for i in order[e] if ops[i]["kind"] == "c"] for e in ENGS}
        sem_of = [None] * len(ops)
        cnt = {e: 0 for e in ENGS}
        duse = [0] * len(self.dsems)
        dnext = {e: 0 for e in ENGS}
        prog = {e: [] for e in ENGS}
        known = {e: {} for e in ENGS}
        prev_on_slot = {}
        for e in ENGS:
            for i in order[e]:
                o = ops[i]
                if o["kind"] == "c":
                    cnt[e] += 1
                    sem_of[i] = (("e", e), cnt[e])
                else:
                    pl = self.dpool[e]
                    sl = pl[dnext[e] % len(pl)]
                    dnext[e] += 1
                    duse[sl] += 1
                    sem_of[i] = (("d", sl), 16 * duse[sl])
                    o["slotprev"] = prev_on_slot.get(sl)
                    prev_on_slot[sl] = i
        out_waits = {}
        for e in ENGS:
            for i in order[e]:
                o = ops[i]
                need = {}
                deps = set(o["deps"])
                if o["kind"] == "d" and o["slotprev"] is not None:
                    deps.add(o["slotprev"])
                for d in deps:
                    kk, v = sem_of[d]
                    if need.get(kk, 0) < v:
                        need[kk] = v
                ws = []
                kn = known[e]
                for kk, v in need.items():
                    if kk == ("e", e) and (e == "pe" or not self.same_engine_sync):
                        continue
                    if kn.get(kk, 0) >= v:
                        continue
                    kn[kk] = v
                    ws.append((kk, v))
                sk, val = sem_of[i]
                prog[e].append((ws, o["fn"], sk, 1 if o["kind"] == "c" else 16))
                if o["kind"] == "d" and o.get("isout"):
                    out_waits[sk] = max(out_waits.get(sk, 0), val)
        prog["sp"].append((list(out_waits.items()), None, None, 0))
        self.prog = prog

        def _sem(k):
            return self.esem[k[1]] if k[0] == "e" else self.dsems[k[1]]

        def make(ename):
            def body(eng):
                for ws, fn, sk, inc in prog[ename]:
                    for kk, v in ws:
                        eng.wait_ge(_sem(kk), v)
                    if fn is None:
                        continue
                    fn(eng).then_inc(_sem(sk), inc)
            return body

        block.tensor(make("pe"))
        block.scalar(make("act"))
        block.vector(make("dve"))
        block.gpsimd(make("pool"))
        block.sync(make("sp"))


W_QA, W_KA, W_VA, W_GA, W_RA, W_QB, W_KB, W_VB, W_OB, W_GB = 0, 256, 512, 1024, 1536, 1568, 1824, 2080, 2592, 3104
NTOK = 1536
NOWN = 768


class K:
    def __init__(self, taps=()):
        self.taps = list(taps)
        self.nc = bass.Bass("TRN2", target_bir_lowering=False)
        self.base_of = {}
        self.sb_off = 16640
        self.st = ExitStack()
        self.dram_in = {}
        self.dram_out = {}
        self.tapouts = {}

    def sb(self, name, shape, dt, off=None):
        sz = int(np.prod(shape[1:])) * _dsize(dt)
        if off is None:
            off = self.sb_off
            self.sb_off = (off + sz + 63) // 64 * 64
        assert off + sz <= 229312, (name, off, sz)
        h = self.nc.alloc_sbuf_tensor_at(name, list(shape), dt, offset=off)
        self.base_of[h.name] = ("SB", off)
        return h

    def din(self, name, shape, dt=F32):
        t = self.nc.dram_tensor(name, list(shape), dt, kind="ExternalInput").ap()
        self.dram_in[name] = t
        return t

    def dout(self, name, shape, dt=F32):
        t = self.nc.dram_tensor(name, list(shape), dt, kind="ExternalOutput").ap()
        self.dram_out[name] = t
        return t

    @staticmethod
    def _n(ap):
        n = 1
        for d in ap.shape[1:]:
            n *= d
        return n

    def _vdur(self, eng, n, k=1.0):
        if eng == "pool":
            return 0.3 + k * n / 500.0
        if eng == "act":
            return 0.25 + k * n / 1200.0
        return 0.19 + k * n / 960.0

    def mm(self, out, lhsT, rhs, start, stop, extra_reads=(), **kw):
        n = self._n(rhs)
        d = 0.03 + max(64, n) / 2400.0 * (4 if rhs.dtype == F32 else 1)
        self.S.op("pe", lambda e: e.matmul(out, lhsT=lhsT, rhs=rhs, start=start, stop=stop, skip_group_check=True, **kw),
                  reads=[lhsT, rhs] + list(extra_reads) + ([] if start else [out]), writes=[out], dur=d)

    def tr(self, out, in_, ident):
        self.S.op("pe", lambda e: e.transpose(out=out, in_=in_, identity=ident), reads=[in_, ident], writes=[out], dur=0.09)

    def act(self, out, in_, func, bias=None, scale=1.0, accum=None, eng="act"):
        rd = [in_]
        kw = {}
        if bias is not None:
            kw["bias"] = bias
            if not isinstance(bias, (int, float)):
                rd.append(bias)
        if not isinstance(scale, (int, float)):
            rd.append(scale)
        wr = [out]
        if accum is not None:
            kw["accum_out"] = accum
            wr.append(accum)
        self.S.op("act", lambda e: e.activation(out=out, in_=in_, func=func, scale=scale, **kw), reads=rd, writes=wr,
                  dur=self._vdur("act", self._n(in_)) + (0.1 if accum is not None else 0.0))

    def tt(self, eng, out, a, b, op):
        self.S.op(eng, lambda e: e.tensor_tensor(out=out, in0=a, in1=b, op=op), reads=[a, b], writes=[out],
                  dur=self._vdur(eng, self._n(a)))

    def ts(self, eng, out, a, s1, op0, s2=None, op1=None, accum=None):
        rd = [a] + [s for s in (s1, s2) if s is not None and not isinstance(s, (int, float))]
        wr = [out] + ([accum] if accum is not None else [])
        kw = {}
        if op1 is not None:
            kw["op1"] = op1
        if accum is not None:
            kw["accum_out"] = accum
        self.S.op(eng, lambda e: e.tensor_scalar(out=out, in0=a, scalar1=s1, scalar2=s2, op0=op0, **kw), reads=rd, writes=wr,
                  dur=self._vdur(eng, self._n(a)))

    def stt(self, out, in0, scalar, in1, op0, op1):
        rd = [in0, in1] + ([] if isinstance(scalar, (int, float)) else [scalar])
        self.S.op("dve", lambda e: e.scalar_tensor_tensor(out=out, in0=in0, scalar=scalar, in1=in1, op0=op0, op1=op1),
                  reads=rd, writes=[out], dur=self._vdur("dve", self._n(in0)))

    def cp(self, eng, out, in_):
        if eng == "act":
            self.S.op("act", lambda e: e.copy(out=out, in_=in_), reads=[in_], writes=[out], dur=self._vdur("act", self._n(in_)))
        else:
            self.S.op(eng, lambda e: e.tensor_copy(out=out, in_=in_), reads=[in_], writes=[out], dur=self._vdur(eng, self._n(in_)))

    def memset(self, eng, out, v):
        self.S.op(eng, lambda e: e.memset(out, v), writes=[out], dur=self._vdur(eng, self._n(out)))

    def scan(self, out, d0, d1, init, op0, op1):
        rd = [d0, d1] + ([] if isinstance(init, (int, float)) else [init])
        self.S.op("dve", lambda e: e.tensor_tensor_scan(out=out, data0=d0, data1=d1, initial=init, op0=op0, op1=op1),
                  reads=rd, writes=[out], dur=self._vdur("dve", self._n(d1), 2.0))

    def reduce(self, out, in_, op, axis=AX.X):
        self.S.op("dve", lambda e: e.tensor_reduce(out=out, in_=in_, axis=axis, op=op), reads=[in_], writes=[out],
                  dur=self._vdur("dve", self._n(in_)))

    def recip(self, out, in_):
        self.S.op("dve", lambda e: e.reciprocal(out=out, in_=in_), reads=[in_], writes=[out], dur=self._vdur("dve", self._n(in_)))

    def dma(self, q, out, in_):
        self.S.dma(q, out, in_)

    def tap(self, name, ap):
        if name not in self.taps:
            return
        shp = list(ap.shape)
        t = self.dout("tap_" + name, shp, ap.dtype)
        self.dma("sp", t, ap)
        self.tapouts[name] = "tap_" + name


def build(taps=(), stage=99):
    k = K(taps)
    nc = k.nc
    st = k.st
    xin = k.din("xin", [NTOK, 1024])
    cond = k.din("cond", [2, 1024])
    sgla = k.din("sgla", [2, 4, 64, 128]); sC = k.din("sC", [2, 4, 64, 128])
    sn = k.din("sn", [2, 4, 64]); sm = k.din("sm", [2, 4])
    tmask = k.din("tmask", [2, 768])
    tmaskT = k.din("tmaskT", [128, 6, 2])
    cmask = k.din("cmask", [2, 1024])
    consts = k.din("consts", [128, 1024])
    w_ada = k.din("w_ada", [1024, 6144]); b_ada = k.din("b_ada", [6144])
    norm1_w = k.din("norm1_w", [1024]); norm2_w = k.din("norm2_w", [1024])
    w_in = k.din("w_in", [1024, 3120])
    w_alpha2 = k.din("w_alpha2", [2, 16, 256]); b_alpha = k.din("b_alpha", [2, 256])
    b_mgate = k.din("b_mgate", [4, 4]); conv_w = k.din("conv_w", [3, 3, 512])
    gnorm_a_w = k.din("gnorm_a_w", [512]); gnorm_b_w = k.din("gnorm_b_w", [512])
    w_out = k.din("w_out", [1024, 1024]); w_ff1 = k.din("w_ff1", [1024, 4096]); w_ff2 = k.din("w_ff2", [4096, 1024])
    final_norm_w = k.din("final_norm_w", [1024])
    yo = k.dout("yo", [NOWN, 1024])
    oS = k.dout("oS", [2, 2, 4, 64, 128]); oC = k.dout("oC", [2, 2, 4, 64, 128])
    on = k.dout("on", [2, 2, 4, 64]); om = k.dout("om", [2, 2, 4])

    PB = []
    for i in range(8):
        h = nc.alloc_psum_tensor("pb%d" % i, [128, 512], F32)
        k.base_of[h.name] = ("PS%d" % i, 0)
        PB.append(h)

    def pbf(i):
        return PB[i][:].bitcast(BF16)

    esems = {e: st.enter_context(nc.semaphore("s_" + e)) for e in ENGS}
    dsems = [st.enter_context(nc.semaphore("d%d" % i)) for i in range(40)]
    S = Sched(nc, esems, dsems, k.base_of)
    k.S = S
    wq = "pool"

    ident_f = k.sb("ident_f", [128, 128], F32)
    ident_b = k.sb("ident_b", [128, 128], BF16)
    cst = k.sb("cst", [128, 1024], F32)
    ones1 = k.sb("ones1", [128, 1], F32)
    eps_t = k.sb("eps_t", [128, 1], F32)
    xres = k.sb("xres", [128, 6, 1024], F32)
    modT = k.sb("modT", [128, 4, 8, 2], F32)
    gb_row = k.sb("gb_row", [128, 2, 2, 1024], F32)
    ssq = k.sb("ssq", [128, 16], F32)
    rstd = k.sb("rstd", [128, 16], F32)
    scondT = k.sb("scondT", [128, 8, 2], BF16)
    g1T = k.sb("g1T", [128, 8], F32); g2T = k.sb("g2T", [128, 8], F32)
    selc = k.sb("selc", [2, 2, 128], F32)
    hT_off = k.sb_off
    hT = k.sb("hT", [128, 8, NTOK], BF16)
    offA = k.sb_off
    k.memset("pool", ident_f[:], 1.0)
    S.op("pool", lambda e: e.affine_select(out=ident_f[:], in_=ident_f[:], pattern=[[-1, 128]], compare_op=ALU.is_equal,
                                           fill=0.0, base=0, channel_multiplier=1), reads=[ident_f[:]], writes=[ident_f[:]])
    k.cp("dve", ident_b[:], ident_f[:])
    k.memset("pool", ones1[:], 1.0)
    k.memset("pool", eps_t[:], 1e-6)
    k.dma("sp", cst[:], consts)
    maskU = cst[:, 0:128]; maskL = cst[:, 128:256]

    def ones_b(p, n):
        return ones1[0:p, 0:1].to_broadcast([p, n])

    S.phase = "B"
    condT = k.sb("condT", [128, 8, 2], F32)
    badd = k.sb("badd", [2, 1024], F32)
    modrow = [k.sb("modrow%d" % i, [2, 1024], F32) for i in range(2)]
    wada = [k.sb("wada%d" % i, [128, 8, 512], BF16) for i in range(3)]
    for c in range(2):
        k.dma("sp", condT[:, :, c], cond[c].rearrange("(kc p) -> p kc", p=128))
    k.dma("sp", g1T[:], norm1_w.rearrange("(kc p) -> p kc", p=128))
    k.dma("sp", g2T[:], norm2_w.rearrange("(kc p) -> p kc", p=128))
    k.act(scondT[:], condT[:], AF.Silu)
    k.cp("dve", selc[:, 0, :], ident_f[0:2, 0:1].to_broadcast([2, 128]))
    k.cp("dve", selc[:, 1, :], ident_f[0:2, 1:2].to_broadcast([2, 128]))
    def mod_piece(piece, wbufs, mr, badd, hw=512, banks=(0, 1, 2, 3), pin=()):
        pTl = PB[banks[2]][:, 256:320].rearrange("p (a b c) -> p a b c", a=4, b=8)
        for hh in range(1024 // hw):
            k.dma("sp", badd[:, hh * hw:(hh + 1) * hw] if badd.shape[1] == 1024 else badd[:, 0:hw],
                  b_ada[piece * 1024 + hh * hw: piece * 1024 + (hh + 1) * hw].partition_broadcast(2))
            wb = wbufs[(2 * (piece - 2) + hh) % len(wbufs)] if piece >= 2 else wbufs[(2 * piece + hh) % len(wbufs)]
            k.dma(wq, wb[:], w_ada[:, piece * 1024 + hh * hw: piece * 1024 + (hh + 1) * hw].rearrange("(kc p) n -> p kc n", p=128))
            ps = PB[banks[hh % 2]][0:2, 0:hw]
            for kc in range(8):
                k.mm(ps, scondT[:, kc, :], wb[:, kc, :], kc == 0, kc == 7, extra_reads=pin if kc == 0 else ())
            k.tt("dve", mr[:, hh * hw:(hh + 1) * hw], ps, badd[:, hh * hw:(hh + 1) * hw] if badd.shape[1] == 1024 else badd[:, 0:hw], ALU.add)
        if piece in (0, 1, 3, 4):
            a = (0, 1, None, 2, 3)[piece]
            for kc in range(8):
                k.tr(pTl[:, a, kc, :], mr[:, kc * 128:(kc + 1) * 128], ident_f[0:2, 0:2])
            k.cp("dve", modT[:, a], pTl[:, a])
            if a in (1, 3):
                for c in range(2):
                    k.stt(modT[:, a, :, c], modT[:, a, :, c], 1.0, (g1T if a == 1 else g2T)[:], ALU.add, ALU.mult)
        else:
            gi = 0 if piece == 2 else 1
            for c in range(2):
                for hh in range(2):
                    ps = PB[banks[3]][:, :]
                    k.mm(ps, selc[:, c, :], mr[:, hh * 512:(hh + 1) * 512], True, True)
                    k.cp("act", gb_row[:, gi, c, hh * 512:(hh + 1) * 512], ps)


    for piece in range(2):
        mod_piece(piece, wada, modrow[piece % 2], badd)
    k.tap("modT", modT[:])
    k.tap("gb_row", gb_row[:])
    if stage <= 1:
        return k

    S.phase = "C"
    xtmp = [k.sb("xtmp%d" % i, [128, 1024], F32) for i in range(3)]
    junk = k.sb("junk", [128, 1024], BF16)
    xn = [k.sb("xn%d" % i, [128, 1024], BF16) for i in range(3)]

    def norm_to_T(xt, si, dstT, tcol, c, a_sh, a_sc, par):
        k.act(junk[:], xt, AF.Square, accum=ssq[:, si:si + 1])
        k.act(rstd[:, si:si + 1], ssq[:, si:si + 1], AF.Sqrt, bias=eps_t[:, 0:1], scale=1.0 / 1024)
        k.recip(rstd[:, si:si + 1], rstd[:, si:si + 1])
        xnb = xn[par]
        k.ts("dve", xnb[:], xt, rstd[:, si:si + 1], ALU.mult)
        bk = ((4, 5), (6, 7), (2, 3))[par]
        pta = pbf(bk[0]).rearrange("p (a b) -> p a b", b=128)
        ptb = pbf(bk[1]).rearrange("p (a b) -> p a b", b=128)
        for kc in range(8):
            k.tr((pta if kc < 4 else ptb)[:, kc % 4, :], xnb[:, kc * 128:(kc + 1) * 128], ident_b[:])
        for kc in range(8):
            if kc < 4:
                k.ts("dve", dstT[:, kc, tcol:tcol + 128], pta[:, kc, :], modT[:, a_sc, kc, c:c + 1], ALU.mult,
                     modT[:, a_sh, kc, c:c + 1], ALU.add)
            else:
                k.act(dstT[:, kc, tcol:tcol + 128], ptb[:, kc - 4, :], AF.Identity, bias=modT[:, a_sh, kc, c:c + 1],
                      scale=modT[:, a_sc, kc, c:c + 1])

    for t in range(12):
        xt = xtmp[t % 3][:]
        k.dma("sp", xt, xin[t * 128:(t + 1) * 128, :])
        norm_to_T(xt, t, hT, t * 128, 0 if t < 4 else 1, 0, 1, t % 3)
    k.tap("hT", hT[:])
    if stage <= 2:
        return k

    k.sb_off = offA
    wst_buf = [k.sb("wblk%d" % i, [128, 8, 512], BF16) for i in range(2)]
    xr_off0 = k.base_of[xres.name][1]
    gb_off0 = k.base_of[gb_row.name][1]
    wst_buf = wst_buf + [k.sb("wblkx%d" % i, [128, 8, 512], BF16, off=gb_off0 + 8192 * i) for i in range(2)]
    wcnt = [0]

    def load_w(src, c0, ncols, buf=None):
        if buf is None:
            b = wst_buf[wcnt[0] % 2]; wcnt[0] += 1
        else:
            b = wst_buf[buf]
        k.dma(wq, b[:, :, 0:ncols], src[:, c0:c0 + ncols].rearrange("(kc p) n -> p kc n", p=128))
        return b

    kT = k.sb("kT", [128, 2, 2, NTOK], BF16)
    qTh = k.sb("qTh", [128, 2, 2, 2, NOWN], BF16)
    v_tok = k.sb("v_tok", [128, 12, 512], BF16)
    vb_aug = k.sb("vb_aug", [128, 12, 4, 130], BF16)
    Ga = k.sb("Ga", [128, 6, 512], BF16)
    Gb = k.sb("Gb", [128, 6, 512], BF16)
    kbTe = k.sb("kbTe", [128, 2, 2, NTOK], BF16)
    qbTh = k.sb("qbTh", [128, 2, 2, NOWN], BF16)
    ebl = k.sb("ebl", [128, 2, 2, 24], F32)
    lbTok = k.sb("lbTok", [128, 6, 36], F32)
    lb8 = k.sb("lb8", [128, 6, 8], F32)
    wbc = k.sb("wbc", [128, 4, 24], F32)
    wbc8 = k.sb("wbc8", [128, 4, 24], F32)
    tmk = k.sb("tmk", [128, 6, 2], F32)
    Mt = k.sb("Mt", [36, 24], F32); mprev = k.sb("mprev", [36, 24], F32); mnew = k.sb("mnew", [36, 24], F32)
    wst = k.sb("wst", [36, 24], F32)
    offB = k.sb_off
    wra = k.sb("wra", [128, 8, 48], BF16)
    wgI = k.sb("wgI", [128, 8, 36], BF16)
    wgF = k.sb("wgF", [128, 8, 36], BF16)
    wa2 = k.sb("wa2", [48, 256], BF16)
    raT = k.sb("raT", [48, NTOK], BF16)
    nba = k.sb("nba", [128, 2, 2], F32)
    bI = k.sb("bI", [36, 1], F32); nbF = k.sb("nbF", [36, 1], F32)
    gnwa = k.sb("gnwa", [128, 512], F32); gnwb = k.sb("gnwb", [128, 512], F32)
    offC = k.sb_off
    k.memset("dve", wra[:], 0.0); k.memset("dve", wgI[:], 0.0); k.memset("dve", wgF[:], 0.0)
    k.memset("dve", wa2[:], 0.0); k.memset("dve", bI[:], 0.0); k.memset("dve", nbF[:], 0.0)
    wv = lambda c0, n: w_in[:, c0:c0 + n].rearrange("(kc p) n -> p kc n", p=128)
    k.dma(wq, wra[:, :, 0:16], wv(W_RA, 16)); k.dma(wq, wra[:, :, 32:48], wv(W_RA + 16, 16))
    k.dma(wq, wgI[:, :, 0:4], wv(W_GB, 4)); k.dma(wq, wgI[:, :, 32:36], wv(W_GB + 8, 4))
    k.dma(wq, wgF[:, :, 0:4], wv(W_GB + 4, 4)); k.dma(wq, wgF[:, :, 32:36], wv(W_GB + 12, 4))
    k.dma(wq, wa2[0:16, :], w_alpha2[0]); k.dma(wq, wa2[32:48, :], w_alpha2[1])
    k.dma("sp", nba[:], b_alpha.rearrange("z (ft p) -> p z ft", p=128))
    k.ts("dve", nba[:], nba[:], -1.0, ALU.mult)
    col = lambda v: v.rearrange("(h o) -> h o", o=1)
    k.dma("sp", bI[0:4, :], col(b_mgate[0])); k.dma("sp", bI[32:36, :], col(b_mgate[2]))
    k.dma("sp", nbF[0:4, :], col(b_mgate[1])); k.dma("sp", nbF[32:36, :], col(b_mgate[3]))
    k.ts("dve", nbF[:], nbF[:], -1.0, ALU.mult)
    k.dma("sp", gnwa[:], gnorm_a_w.partition_broadcast(128)); k.dma("sp", gnwb[:], gnorm_b_w.partition_broadcast(128))
    k.dma("sp", tmk[:], tmaskT)
    k.memset("pool", vb_aug[:, :, :, 128:130], 1.0)
    k.memset("pool", qTh[:], 0.0)
    k.memset("pool", qbTh[:], 0.0)

    TB = [(0, 512), (512, 512), (1024, 512)]
    OB = [(0, 512), (512, 256)]
    bank = [0]

    def nextbank():
        b = bank[0]; bank[0] = (b + 1) % 4
        return PB[b]

    def fproj(lhs_of_kc, M, t0, n):
        ps = nextbank()[0:M, 0:n]
        for kc in range(8):
            k.mm(ps, lhs_of_kc(kc), hT[:, kc, t0:t0 + n], kc == 0, kc == 7)
        return ps

    S.phase = "D1"
    k.sb_off = offC
    iT = k.sb("iT", [36, NTOK], F32)
    lfp = k.sb("lfp", [36, NTOK], F32)
    for (t0, n) in TB:
        ps = fproj(lambda kc: wra[:, kc, :], 48, t0, n)
        k.cp("act", raT[:, t0:t0 + n], ps)
        ps = fproj(lambda kc: wgI[:, kc, :], 36, t0, n)
        k.act(iT[:, t0:t0 + n], ps, AF.Identity, bias=bI[:, 0:1])
        ps = fproj(lambda kc: wgF[:, kc, :], 36, t0, n)
        k.act(lfp[:, t0:t0 + n], ps, AF.Exp, bias=nbF[:, 0:1], scale=-1.0)
    k.act(lfp[:], lfp[:], AF.Ln, bias=1.0)
    k.tap("raT", raT[:]); k.tap("iT", iT[:]); k.tap("lfp", lfp[:])
    if stage <= 3:
        return k
    offD = k.sb_off
    d4_w = {"va": load_w(w_in, W_VA, 512, buf=2), "vb": load_w(w_in, W_VB, 512, buf=3)}

    S.phase = "D5"
    k.sb_off = offD
    gm = k.sb("gm", [36, 2, 768], BF16)
    Gg = k.sb("Gg", [36, NTOK], F32)
    Xs = k.sb("Xs", [36, NTOK], F32)
    Ggprev = k.sb("Ggprev", [36, 24, 1], F32)
    totp = k.sb("totp", [36, 24], F32)
    amax = k.sb("amax", [36, 24], F32)
    dirm = k.sb("dirm", [36, 1], F32)
    smT = k.sb("smT", [36, 1], F32)
    zero2 = k.sb("zero2", [36, 2], F32)
    Tinc = k.sb("Tinc", [36, 16], F32); Tprv = k.sb("Tprv", [36, 16], F32)
    bsc = k.sb("bsc", [36, 16], F32); mpr = k.sb("mpr", [36, 16], F32)
    k.memset("pool", gm[:, 0, :], 1.0)
    k.dma(wq, gm[0:4, 0, :], tmask[0].partition_broadcast(4))
    k.dma(wq, gm[32:36, 0, :], tmask[1].partition_broadcast(4))
    k.ts("dve", gm[:, 1, :], gm[:, 0, :], -1.0, ALU.add, BIG, ALU.mult)
    k.memset("pool", dirm[:], 0.0); k.memset("pool", dirm[32:36, :], 1.0)
    k.memset("pool", zero2[:], 0.0); k.memset("pool", smT[:], 0.0)
    k.dma("sp", smT[0:4, :], col(sm[0])); k.dma("sp", smT[32:36, :], col(sm[1]))
    k.memset("pool", Ggprev[:, 0:1, :], 0.0)
    k.memset("pool", Mt[:], 0.0); k.memset("pool", wst[:], 0.0); k.memset("pool", mnew[:], 0.0)
    k.tt("dve", lfp[:, 768:], lfp[:, 768:], gm[:, 0, :], ALU.mult)
    k.scan(Gg[:], ones_b(36, NTOK), lfp[:], 0.0, ALU.mult, ALU.add)
    Gg3 = Gg[:].rearrange("p (c j) -> p c j", j=64)
    X3 = Xs[:].rearrange("p (c j) -> p c j", j=64)
    lf3 = lfp[:].rearrange("p (c j) -> p c j", j=64)
    i3 = iT[:].rearrange("p (c j) -> p c j", j=64)
    Ggend = Gg3[:, :, 63:64]
    k.cp("pool", Ggprev[:, 1:24, :], Ggend[:, 0:23, :])
    k.tt("dve", totp[:], Ggend[:, :, 0], Ggprev[:, :, 0], ALU.subtract)
    k.tt("dve", X3, lf3, Gg3, ALU.subtract)
    k.tt("dve", X3, X3, Ggend.to_broadcast([36, 24, 64]), ALU.add)
    k.tt("dve", Gg3, Gg3, Ggprev[:].to_broadcast([36, 24, 64]), ALU.subtract)
    k.tt("dve", Xs[:], Xs[:], Gg[:], ALU.subtract)
    k.stt(Gg[:], Xs[:], dirm[:, 0:1], Gg[:], ALU.mult, ALU.add)
    k.tt("dve", iT[:], iT[:], Gg[:], ALU.add)
    k.tt("dve", iT[:, 768:], iT[:, 768:], gm[:, 0, :], ALU.mult)
    k.tt("dve", iT[:, 768:], iT[:, 768:], gm[:, 1, :], ALU.add)
    k.reduce(amax[:], i3, ALU.max)
    vP = lambda tl: tl[:, 0:8].rearrange("p (s c) -> p s c", c=4)
    for rows, fwd in ((slice(0, 4), True), (slice(32, 36), False)):
        prev = zero2[rows, :]
        for c in (range(4) if fwd else range(3, -1, -1)):
            k.tt("dve", vP(Mt)[rows, :, c], prev, vP(amax)[rows, :, c], ALU.max)
            k.tt("dve", vP(wst)[rows, :, c], prev, vP(Mt)[rows, :, c], ALU.subtract)
            k.tt("dve", vP(mnew)[rows, :, c], vP(Mt)[rows, :, c], vP(totp)[rows, :, c], ALU.subtract)
            prev = vP(mnew)[rows, :, c]
        prev = smT[rows, :]
        for (lo, hi) in ((12, 24), (8, 12)):
            n_ = hi - lo
            wv = (lambda tl: tl[rows, lo:hi]) if fwd else (lambda tl: tl[rows, lo:hi][:, ::-1])
            k.scan(Tinc[rows, 0:n_], ones_b(36, n_)[rows, :], wv(totp), 0.0, ALU.mult, ALU.add)
            k.tt("dve", Tprv[rows, 0:n_], Tinc[rows, 0:n_], wv(totp), ALU.subtract)
            k.tt("dve", bsc[rows, 0:n_], wv(amax), Tprv[rows, 0:n_], ALU.add)
            k.scan(mpr[rows, 1:n_ + 1], bsc[rows, 0:n_], bsc[rows, 0:n_], prev, ALU.max, ALU.max)
            k.cp("dve", mpr[rows, 0:1], prev)
            k.tt("dve", wv(Mt), mpr[rows, 1:n_ + 1], Tprv[rows, 0:n_], ALU.subtract)
            k.tt("dve", wv(mnew), mpr[rows, 1:n_ + 1], Tinc[rows, 0:n_], ALU.subtract)
            k.tt("dve", wv(wst), mpr[rows, 0:n_], mpr[rows, 1:n_ + 1], ALU.subtract)
            last = hi - 1 if fwd else lo
            prev = mnew[rows, last:last + 1]
    Mt3 = Mt[:].rearrange("p (c o) -> p c o", o=1)
    k.tt("dve", i3, i3, Mt3.to_broadcast([36, 24, 64]), ALU.subtract)
    k.act(iT[:], iT[:], AF.Exp)
    k.act(wst[:], wst[:], AF.Exp)
    k.tt("dve", Gg3[:, 0:12, :], Gg3[:, 0:12, :], Mt3[:, 0:12, :].to_broadcast([36, 12, 64]), ALU.subtract)
    k.act(Gg[:, 0:NOWN], Gg[:, 0:NOWN], AF.Exp)
    pl_ = PB[5][:, 0:216].rearrange("p (t r) -> p t r", r=36)
    for t in range(6):
        k.tr(pl_[:, t, :], Gg[0:36, t * 128:(t + 1) * 128], ident_f[0:36, 0:36])
    k.cp("dve", lbTok[:], pl_)
    k.cp("dve", lb8[:, :, 0:4], lbTok[:, :, 0:4]); k.cp("dve", lb8[:, :, 4:8], lbTok[:, :, 32:36])
    pw_ = PB[6][:, 0:96].rearrange("p (a c) -> p a c", c=24)
    for a in range(4):
        k.mm(pw_[:, a, :], cst[0:36, 256 + a * 128: 256 + (a + 1) * 128], wst[0:36, :], True, True)
    k.cp("dve", wbc[:], pw_)
    k.ts("dve", wbc8[:], wbc[:], 0.125, ALU.mult)
    for sq in range(2):
        k.dma("sp", om[sq, 0].rearrange("(h o) -> h o", o=1), mnew[0:4, 4 * sq + 3: 4 * sq + 4])
        k.dma("sp", om[sq, 1].rearrange("(h o) -> h o", o=1), mnew[32:36, 4 * sq: 4 * sq + 1])
    k.tap("eT", iT[:]); k.tap("lbTok", lbTok[:]); k.tap("wbc", wbc[:]); k.tap("mnew", mnew[:]); k.tap("Mt", Mt[:])
    if stage <= 6:
        return k

    S.phase = "D6"
    k.sb_off = offC + 6144
    cmb = k.sb("cmb", [128, 2, 1024], BF16)
    cwt = k.sb("cwt", [128, 4, 9], F32)
    dgs = [k.sb("dg%d" % i, [128, 9, 128], BF16) for i in range(2)]
    Up = k.sb("Up", [128, 2, 258], BF16)
    Us = [k.sb("Us%d" % i, [128, 18, 66], BF16) for i in range(3)]
    kbT = k.sb("kbT", [128, 2, NTOK], BF16, off=xr_off0 + 16384)
    for j_ in range(2):
        k.dma(wq, cmb[:, j_, :], cmask[j_].partition_broadcast(128))
    for f4 in range(4):
        k.dma("sp", cwt[:, f4, :], conv_w[:, :, f4 * 128:(f4 + 1) * 128].rearrange("a b p -> p (a b)"))
    wqb = load_w(w_in, W_QB, 512)
    cm3 = lambda j, R: cmb[:, j, 0:R * 64].rearrange("p (r x) -> p r x", x=64)
    k.memset("pool", Up[:], 0.0)
    for u_ in Us:
        k.memset("pool", u_[:], 0.0)
    for f4 in range(4):
        isq = f4 < 2
        wl = lambda kc: wqb[:, kc, f4 * 128:(f4 + 1) * 128]
        dg = dgs[f4 % 2]
        for tp_ in range(9):
            k.ts("pool", dg[:, tp_, :], ident_f[:], cwt[:, f4, tp_:tp_ + 1], ALU.mult, 1.0, ALU.mult)
        ps = fproj(wl, 128, 0, 512)
        k.cp("act", Up[:, :, 1:257], ps.rearrange("p (s t) -> p s t", s=2))
        if isq:
            ps = fproj(wl, 128, 512, 320)
            k.cp("act", Us[1][:, 1:6, 1:65], ps.rearrange("p (r x) -> p r x", x=64))
            ps = fproj(wl, 128, 1472, 64)
            k.cp("act", Us[1][:, 0:1, 1:65], ps.rearrange("p (r x) -> p r x", x=64))
            R = 4
            blocks = [(0, 4)]
        else:
            for hb in range(2):
                ps = fproj(wl, 128, 512 + hb * 512, 512)
                k.cp("act", Us[1][:, 1 + 8 * hb: 9 + 8 * hb, 1:65], ps.rearrange("p (r x) -> p r x", x=64))
            k.cp("pool", Us[1][:, 0:1, 1:65], Us[1][:, 16:17, 1:65])
            k.cp("pool", Us[1][:, 17:18, 1:65], Us[1][:, 1:2, 1:65])
            R = 16
            blocks = [(0, 8), (8, 8)]
        k.tt("dve", Us[0][:, 0:R, 1:65], Us[1][:, 0:R, 1:65], cm3(0, R), ALU.mult)
        k.tt("dve", Us[2][:, 2:2 + R, 1:65], Us[1][:, 2:2 + R, 1:65], cm3(1, R), ALU.mult)
        ps = nextbank()[:, 0:512].rearrange("p (s t) -> p s t", s=2)
        for b in range(3):
            k.mm(ps, dg[:, 3 + b, :], Up[:, :, b:b + 256], b == 0, b == 2)
        psf = ps.rearrange("p s t -> p (s t)")
        if isq:
            for h2 in range(2):
                rw = slice(64 * h2, 64 * h2 + 64)
                k.act(qbTh[rw, f4, h2, 0:512], psf[rw], AF.Silu)
        else:
            k.act(kbT[:, f4 % 2, 0:512], psf, AF.Silu)
        for (r0, rb) in blocks:
            ps = nextbank()[:, 0:rb * 64].rearrange("p (r x) -> p r x", x=64)
            for tp_ in range(9):
                a, b = tp_ // 3, tp_ % 3
                k.mm(ps, dg[:, tp_, :], Us[a][:, r0 + a:r0 + a + rb, b:b + 64], tp_ == 0, tp_ == 8)
            psf = ps.rearrange("p r x -> p (r x)")
            c0 = 512 + r0 * 64
            if isq:
                for h2 in range(2):
                    rw = slice(64 * h2, 64 * h2 + 64)
                    k.act(qbTh[rw, f4, h2, c0:c0 + rb * 64], psf[rw], AF.Silu)
            else:
                k.act(kbT[:, f4 % 2, c0:c0 + rb * 64], psf, AF.Silu)
    k.tap("kbT", kbT[:])
    S.phase = "D23"
    k.sb_off = offC + 6144
    tseg = k.sb("tseg", [128, 2, 3], F32)
    Lb = k.sb("Lb", [128, NTOK], F32)
    Gs = k.sb("Gs", [128, NTOK], F32)
    Ep = k.sb("Ep", [128, NOWN], F32)
    kraw = k.sb("kraw", [128, NTOK], F32)
    qraw = k.sb("qraw", [128, NOWN], F32)
    Gprev = k.sb("Gprev", [128, 24, 1], F32)
    tot = k.sb("tot", [128, 24, 1], F32)
    for z in range(2):
        k.dma("sp", tseg[:, z, :], tmask[z].rearrange("(s t) -> s t", t=256)[:, 0].partition_broadcast(128))
    wqk = load_w(w_in, 0, 512)
    Lb_b = k.sb("Lb_b", [128, NTOK], F32, off=xr_off0)
    Gs_b = k.sb("Gs_b", [128, NTOK], F32, off=xr_off0 + 6144)
    Ep_b = k.sb("Ep_b", [128, NOWN], F32, off=xr_off0 + 12288)
    Gprev_b = k.sb("Gprev_b", [128, 24, 1], F32, off=xr_off0 + 15360)
    tot_b = k.sb("tot_b", [128, 24, 1], F32, off=xr_off0 + 15488)
    k.memset("pool", Gprev[:, 0:1, :], 0.0)
    k.memset("pool", Gprev_b[:, 0:1, :], 0.0)
    D23SETS = [(Lb, Gs, Ep, Gprev, tot), (Lb_b, Gs_b, Ep_b, Gprev_b, tot_b)]
    for ft in range(2):
        for (t0, n) in TB:
            ps = fproj(lambda kc: wqk[:, kc, 256 + ft * 128: 256 + (ft + 1) * 128], 128, t0, n)
            k.cp("act", kraw[:, t0:t0 + n], ps)
        for (t0, n) in OB:
            ps = fproj(lambda kc: wqk[:, kc, ft * 128:(ft + 1) * 128], 128, t0, n)
            k.act(qraw[:, t0:t0 + n], ps, AF.Copy, scale=0.125)
        for z in range(2):
            Lb, Gs, Ep, Gprev, tot = D23SETS[z]
            L3 = Lb[:].rearrange("p (c j) -> p c j", j=64)
            G3 = Gs[:].rearrange("p (c j) -> p c j", j=64)
            Gend = G3[:, :, 63:64]
            for (t0, n) in TB:
                ps = nextbank()[:, 0:n]
                k.mm(ps, wa2[32 * z:32 * z + 16, ft * 128:(ft + 1) * 128], raT[32 * z:32 * z + 16, t0:t0 + n], True, True)
                k.act(Lb[:, t0:t0 + n], ps, AF.Exp, bias=nba[:, z, ft:ft + 1], scale=-1.0)
            k.act(Lb[:], Lb[:], AF.Ln, bias=1.0)
            for sg in range(3):
                k.ts("dve", Lb[:, 768 + 256 * sg:1024 + 256 * sg], Lb[:, 768 + 256 * sg:1024 + 256 * sg], tseg[:, z, sg:sg + 1], ALU.mult)
            k.scan(Gs[:], ones_b(128, NTOK), Lb[:], 0.0, ALU.mult, ALU.add)
            k.cp("pool", Gprev[:, 1:24, :], Gend[:, 0:23, :])
            k.tt("dve", tot[:], Gend, Gprev[:], ALU.subtract)
            k.act(ebl[:, z, ft, :], tot[:, :, 0], AF.Exp, scale=-1.0 / 16)
            if z == 0:
                k.tt("dve", G3, G3, Gprev[:].to_broadcast([128, 24, 64]), ALU.subtract)
                Pm = Gs
            else:
                k.tt("dve", L3, L3, G3, ALU.subtract)
                k.tt("dve", L3, L3, Gend.to_broadcast([128, 24, 64]), ALU.add)
                Pm = Lb
            k.act(Ep[:], Pm[:, 0:NOWN], AF.Exp, scale=-1.0 / 16)
            k.act(Pm[:], Pm[:], AF.Exp, scale=1.0 / 16)
            k.tt("pool", kT[:, z, ft, :], kraw[:], Pm[:], ALU.mult)
            k.tt("dve", qTh[0:64, z, ft, 0, :], qraw[0:64, :], Ep[0:64, :], ALU.mult)
            k.tt("dve", qTh[64:128, z, ft, 1, :], qraw[64:128, :], Ep[64:128, :], ALU.mult)
    k.tap("kT", kT[:]); k.tap("ebl", ebl[:])
    if stage <= 4:
        return k

    S.phase = "D7"
    for z in range(2):
        for pr in range(2):
            a = 2 * z + pr
            for (t0, n) in TB:
                ps = nextbank()[:, 0:n]
                k.mm(ps, cst[0:36, 256 + a * 128: 256 + (a + 1) * 128], iT[0:36, t0:t0 + n], True, True)
                k.tt("dve", kbTe[:, z, pr, t0:t0 + n], kbT[:, pr, t0:t0 + n], ps, ALU.mult)
    k.tap("kbTe", kbTe[:])
    if stage <= 7:
        return k

    S.phase = "D4"
    k.sb_off = offD
    gtmp = [k.sb("gtmp%d" % i, [128, 512], F32) for i in range(2)]
    d4bank = [0]
    for (c0, kind) in ((W_VA, "va"), (W_VB, "vb"), (W_GA, "ga"), (W_OB, "ob")):
        wb = d4_w[kind] if kind in d4_w else load_w(w_in, c0, 512, buf=2 if kind == "ga" else 3)
        for t in range(12 if kind in ("va", "vb") else 6):
            ps = PB[4 + d4bank[0] % 4][:, :]; d4bank[0] += 1
            for kc in range(8):
                k.mm(ps, hT[:, kc, t * 128:(t + 1) * 128], wb[:, kc, :], kc == 0, kc == 7)
            if kind == "va":
                k.cp("act", v_tok[:, t, :], ps)
            elif kind == "vb":
                k.cp("act", vb_aug[:, t, :, 0:128], ps.rearrange("p (h v) -> p h v", h=4))
            elif kind == "ga":
                k.act(gtmp[t % 2][:], ps, AF.Silu)
                k.tt("dve", Ga[:, t, :], gtmp[t % 2][:], gnwa[:], ALU.mult)
            else:
                k.act(gtmp[t % 2][:], ps, AF.Sigmoid)
                k.tt("dve", Gb[:, t, :], gtmp[t % 2][:], gnwb[:], ALU.mult)
    k.tap("v_tok", v_tok[:]); k.tap("vb_aug", vb_aug[:]); k.tap("Ga", Ga[:]); k.tap("Gb", Gb[:])
    if stage <= 5:
        return k

    wo = k.sb("wo", [128, 8, 1024], BF16, off=offA)
    for hh in range(2):
        k.dma(wq, wo[:, :, hh * 512:(hh + 1) * 512], w_out[:, hh * 512:(hh + 1) * 512].rearrange("(kc p) n -> p kc n", p=128))
    S.phase = "E"
    k.sb_off = offB
    ktGM = [k.sb("ktGM%d" % i, [128, 2, 2, 4, 128], BF16) for i in range(2)]
    Sst0 = k.sb("Sst", [128, 2, 2, 128], F32)
    Stmp0 = k.sb("Stmp", [128, 2, 2, 128], F32)
    CNst0 = k.sb("CNst", [128, 2, 2, 130], F32)
    Ssn0 = k.sb("Ssn", [128, 2, 2, 4, 128], BF16)
    CNsn0 = k.sb("CNsn", [128, 2, 2, 4, 130], BF16)
    xo = k.base_of[xres.name][1]
    Sst1 = k.sb("Sst1", [128, 2, 2, 128], F32, off=xo)
    Stmp1 = k.sb("Stmp1", [128, 2, 2, 128], F32, off=xo + 2048)
    CNst1 = k.sb("CNst1", [128, 2, 2, 130], F32, off=xo + 4096)
    Ssn1 = k.sb("Ssn1", [128, 2, 2, 4, 128], BF16, off=xo + 6272)
    CNsn1 = k.sb("CNsn1", [128, 2, 2, 4, 130], BF16, off=xo + 10368)
    ESETS = [(Sst0, Stmp0, CNst0, Ssn0, CNsn0), (Sst1, Stmp1, CNst1, Ssn1, CNsn1)]
    scbG = [k.sb("scbG%d" % i, [128, 2, 2, 4, 64], BF16) for i in range(2)]
    scbM = [k.sb("scbM%d" % i, [128, 2, 2, 4, 64], BF16) for i in range(2)]
    for i in range(2):
        k.memset("pool", ktGM[i][:], 0.0)
        k.memset("pool", scbG[i][:], 0.0); k.memset("pool", scbM[i][:], 0.0)
    mask2 = k.sb("mask2", [128, 2, 64], F32)
    mask2s = k.sb("mask2s", [128, 2, 64], F32)
    hstat = k.sb("hstat", [128, 16], F32)
    dm8 = k.sb("dm8", [128, 8], F32)
    hsum = k.sb("hsum", [128, 512], F32)
    mixb = [k.sb("mixb%d" % i, [128, 1024], BF16) for i in range(2)]
    mixT_h = k.sb("mixT", [128, 8, NOWN], BF16, off=hT_off)
    mixT = mixT_h[:]
    for par in range(2):
        k.cp("pool", mask2[64 * par:64 * par + 64, 0, :], maskU[64 * par:64 * par + 64, 64 * par:64 * par + 64])
        k.cp("pool", mask2[64 * par:64 * par + 64, 1, :], maskL[64 * par:64 * par + 64, 64 * par:64 * par + 64])
    k.ts("dve", mask2s[:], mask2[:], 0.125, ALU.mult)
    stateview = lambda d: d.rearrange("(pr h2) d v -> (h2 d) pr v", pr=2)

    def seq_scan(sq):
        Sst, Stmp, CNst, Ssn, CNsn = ESETS[sq % 2]
        sample = sq == 2
        own_tiles = [2 * sq, 2 * sq + 1]
        c0 = 4 * sq
        if sample:
            orders = [list(range(12, 24)) + list(range(8, 12)), list(range(23, 11, -1)) + list(range(11, 7, -1))]
            tiles = list(range(6, 12)) + own_tiles
        else:
            orders = [list(range(c0, c0 + 4)), list(range(c0 + 3, c0 - 1, -1))]
            tiles = own_tiles
        S.phase = "Ec%d" % sq
        if sample:
            for z in range(2):
                k.dma("sp", Sst[:, z, :, :], stateview(sgla[z]))
                k.dma("sp", CNst[:, z, :, 0:128], stateview(sC[z]))
                k.dma("sp", CNst[:, z, :, 128], sn[z].rearrange("(pr h2) d -> (h2 d) pr", pr=2))
            k.memset("pool", CNst[:, :, :, 129:130], 0.0)
        else:
            k.memset("pool", Sst[:], 0.0)
            k.memset("pool", CNst[:], 0.0)
        ktile = [None, None]
        kcnt = [0, 0]

        def get_kt(t, z):
            if ktile[z] is not None and ktile[z][0] == t:
                return ktile[z][1], ktile[z][2]
            i = kcnt[z] % 2; kcnt[z] += 1
            pt = pbf(4).rearrange("p (a b) -> p a b", b=128)
            for ft in range(2):
                k.tr(pt[:, 4 * z + ft, :], kT[:, z, ft, t * 128:(t + 1) * 128], ident_b[:])
                k.tr(pt[:, 4 * z + 2 + ft, :], kbTe[:, z, ft, t * 128:(t + 1) * 128], ident_b[:])
            for par in range(2):
                rw = slice(64 * par, 64 * par + 64)
                if t >= 6:
                    k.act(ktGM[i][rw, par, z, :, :], pt[rw, 4 * z:4 * z + 4, :], AF.Copy, scale=tmk[rw, t - 6, z:z + 1])
                else:
                    k.cp("act", ktGM[i][rw, par, z, :, :], pt[rw, 4 * z:4 * z + 4, :])
            ktile[z] = (t, ktGM[i], ktGM[i])
            return ktGM[i], ktGM[i]

        pend = [None, None]
        for step in range(len(orders[0])):
            for z in range(2):
                c = orders[z][step]
                t = c // 2; par = c % 2
                kg, km = get_kt(t, z)
                own = (c // 4) == (sq if not sample else 2)
                cl = c - c0 if not sample else c - 8
                rows = slice(64 * par, 64 * par + 64)
                psS = PB[0][:, 256 * z:256 * z + 256].rearrange("p (f v) -> p f v", f=2)
                for ft in range(2):
                    for h2 in range(2):
                        k.mm(psS[64 * h2:64 * h2 + 64, ft, :], kg[:, par, z, ft, 64 * h2:64 * h2 + 64],
                             v_tok[:, t, (2 * ft + h2) * 128:(2 * ft + h2 + 1) * 128], True, True)
                for ft in range(2):
                    sc_prev = ones1[:, 0:1] if pend[z] is None else ebl[:, z, ft, pend[z]:pend[z] + 1]
                    if own:
                        k.act(Ssn[:, z, ft, cl, :], Sst[:, z, ft, :], AF.Copy, scale=sc_prev)
                    k.stt(Sst[:, z, ft, :], Sst[:, z, ft, :], sc_prev, psS[:, ft, :], ALU.mult, ALU.add)
                pend[z] = c
                psC = PB[2 + z][:, 0:260].rearrange("p (f v) -> p f v", f=2)
                for pr in range(2):
                    for h2 in range(2):
                        k.mm(psC[64 * h2:64 * h2 + 64, pr, 0:129], km[:, par, z, 2 + pr, 64 * h2:64 * h2 + 64],
                             vb_aug[:, t, 2 * pr + h2, 0:129], True, True)
                for pr in range(2):
                    if own:
                        k.act(CNsn[:, z, pr, cl, :], CNst[:, z, pr, :], AF.Copy, scale=wbc8[:, 2 * z + pr, c:c + 1])
                    k.stt(CNst[:, z, pr, 0:129], CNst[:, z, pr, 0:129], wbc[:, 2 * z + pr, c:c + 1], psC[:, pr, 0:129], ALU.mult, ALU.add)
        if not sample:
            for z in range(2):
                k.tt("dve", Sst[:, z], Sst[:, z], ebl[:, z, :, pend[z]:pend[z] + 1].to_broadcast([128, 2, 128]), ALU.mult)
                k.dma("sp", stateview(oS[sq, z]), Sst[:, z, :, :])
                k.dma("sp", stateview(oC[sq, z]), CNst[:, z, :, 0:128])
                k.dma("sp", on[sq, z].rearrange("(pr h2) d -> (h2 d) pr", pr=2), CNst[:, z, :, 128])
        S.phase = "Eo%d" % sq
        for ti, t in enumerate(own_tiles):
            i = ti % 2
            tok0 = t * 128
            scG = PB[5][:, :].rearrange("p (z h x) -> p z h x", z=2, h=4)
            scM = PB[6][:, :].rearrange("p (z h x) -> p z h x", z=2, h=4)
            for z in range(2):
                for h in range(4):
                    fr = slice(64 * (h % 2), 64 * (h % 2) + 64)
                    for par in range(2):
                        tk = slice(tok0 + 64 * par, tok0 + 64 * par + 64)
                        rows = slice(64 * par, 64 * par + 64)
                        k.mm(scG[rows, z, h, :], kT[:, z, h // 2, tk], qTh[:, z, h // 2, h % 2, tk], True, True)
                        k.mm(scM[rows, z, h, :], kbTe[:, z, h // 2, tk], qbTh[:, h // 2, h % 2, tk], True, True)
            for z in range(2):
                for par in range(2):
                    rw = slice(64 * par, 64 * par + 64)
                    k.tt("dve", scbG[i][rw, par, z], scG[rw, z], mask2[rw, z:z + 1, :].to_broadcast([64, 4, 64]), ALU.mult)
                    k.tt("dve", scbM[i][rw, par, z], scM[rw, z], mask2s[rw, z:z + 1, :].to_broadcast([64, 4, 64]), ALU.mult)
            oG = PB[7][:, :]

            def ndslot(s_):
                return PB[(0, 2, 3)[s_ // 3]][:, (s_ % 3) * 129:(s_ % 3) * 129 + 129]
            for par in range(2):
                rows = slice(64 * par, 64 * par + 64)
                tk = slice(tok0 + 64 * par, tok0 + 64 * par + 64)
                cl = 2 * ti + par
                for h in range(4):
                    fr = slice(64 * (h % 2), 64 * (h % 2) + 64)
                    og = oG[rows, 128 * h:128 * h + 128]
                    k.mm(og, scbG[i][:, par, 0, h, :], v_tok[:, t, 128 * h:128 * h + 128], True, False)
                    k.mm(og, qTh[:, 0, h // 2, h % 2, tk], Ssn[:, 0, h // 2, cl, :], False, False)
                    k.mm(og, scbG[i][:, par, 1, h, :], v_tok[:, t, 128 * h:128 * h + 128], False, False)
                    k.mm(og, qTh[:, 1, h // 2, h % 2, tk], Ssn[:, 1, h // 2, cl, :], False, True)
                    for z in range(2):
                        nd = ndslot(4 * z + h)[rows, :]
                        k.mm(nd, scbM[i][:, par, z, h, :], vb_aug[:, t, h, 0:129], True, False)
                        k.mm(nd, qbTh[:, h // 2, h % 2, tk], CNsn[:, z, h // 2, cl, 0:129], False, True)
            mx = mixb[i]
            for h in range(4):
                k.act(hsum[:, 128 * h:128 * h + 128], oG[:, 128 * h:128 * h + 128], AF.Square, accum=hstat[:, h:h + 1])
            k.act(hstat[:, 0:4], hstat[:, 0:4], AF.Sqrt, bias=eps_t[:, 0:1], scale=1.0 / 128)
            k.recip(hstat[:, 0:4], hstat[:, 0:4])
            for h in range(4):
                k.stt(mx[:, 128 * h:128 * h + 128], oG[:, 128 * h:128 * h + 128], hstat[:, h:h + 1],
                      Ga[:, t, 128 * h:128 * h + 128], ALU.mult, ALU.mult)
            for bk, (s0, ns) in enumerate(((0, 3), (3, 3), (6, 2))):
                den = PB[(0, 2, 3)[bk]][:, 0:ns * 129].rearrange("p (s v) -> p s v", v=129)[:, :, 128]
                k.act(dm8[:, s0:s0 + ns], den, AF.Abs)
            k.tt("dve", dm8[:], dm8[:], lb8[:, t, :], ALU.max)
            k.recip(dm8[:], dm8[:])
            for h in range(4):
                k.act(hsum[:, 128 * h:128 * h + 128], ndslot(h)[:, 0:128], AF.Copy, scale=dm8[:, h:h + 1])
            for h in range(4):
                k.stt(hsum[:, 128 * h:128 * h + 128], ndslot(4 + h)[:, 0:128], dm8[:, 4 + h:5 + h],
                      hsum[:, 128 * h:128 * h + 128], ALU.mult, ALU.add)
            for h in range(4):
                k.act(mx[:, 512 + 128 * h:512 + 128 * h + 128], hsum[:, 128 * h:128 * h + 128], AF.Square, accum=hstat[:, 4 + h:5 + h])
            k.act(hstat[:, 4:8], hstat[:, 4:8], AF.Sqrt, bias=eps_t[:, 0:1], scale=1.0 / 128)
            k.recip(hstat[:, 4:8], hstat[:, 4:8])
            for h in range(4):
                k.stt(mx[:, 512 + 128 * h:512 + 128 * h + 128], hsum[:, 128 * h:128 * h + 128], hstat[:, 4 + h:5 + h],
                      Gb[:, t, 128 * h:128 * h + 128], ALU.mult, ALU.mult)
            pta = pbf(4).rearrange("p (a b) -> p a b", b=128)
            for kc in range(8):
                k.tr(pta[:, kc, :], mx[:, kc * 128:(kc + 1) * 128], ident_b[:])
            k.cp("act", mixT[:, :, tok0:tok0 + 128], pta[:, :, :])

    for sq in range(3):
        seq_scan(sq)
    k.tap("mixT", mixT)
    if stage <= 8:
        return k

    S.phase = "Blate"
    soffL = hT_off + 8 * NOWN * 2
    g2_off = k.base_of[gb_row.name][1] + 8192
    xr_off = k.base_of[xres.name][1]
    wadaL = [k.sb("wadaL0", [128, 8, 512], BF16, off=soffL), k.sb("wadaL1", [128, 8, 512], BF16, off=g2_off)]
    mrL = k.sb("mrL", [2, 1024], F32, off=soffL + 8192)
    baddL = k.sb("baddL", [2, 512], F32, off=229376 - 2112)
    for piece in (2, 3, 4):
        mod_piece(piece, wadaL, mrL, baddL, hw=512, banks=(1, 1, 1, 1), pin=[mixT[:, 0, 128:256]])
    S.phase = "cast"
    sc1 = nc.dram_tensor("sc_ff1", [8, 128, 4096], BF16, kind="Internal").ap()
    sc2 = nc.dram_tensor("sc_ff2", [8, 128, 4096], BF16, kind="Internal").ap()
    S.tracked_dram.add(sc1.tensor.name); S.tracked_dram.add(sc2.tensor.name)
    for fb in range(8):
        k.dma(wq, sc1[fb].rearrange("p (kc n) -> p kc n", kc=8), w_ff1[:, fb * 512:(fb + 1) * 512].rearrange("(kc p) n -> p kc n", p=128))
        k.dma(wq, sc2[fb].rearrange("p (ft n) -> p ft n", ft=4), w_ff2[512 * fb:512 * (fb + 1), :].rearrange("(ft p) n -> p ft n", p=128))
    S.phase = "F"
    soff = hT_off + 8 * NOWN * 2
    ytmp = k.sb("ytmp", [128, 1024], F32, off=soff)
    junk2 = k.sb("junk2", [128, 1024], BF16, off=soff + 4096)
    xn2 = [k.sb("xn2_%d" % i, [128, 1024], BF16, off=soff + 6144 + 2048 * i) for i in range(2)]
    h2T = mixT_h
    junk_sv, xn_sv = junk, xn

    def norm_to_T2(xt, si, dstT, tcol, c, a_sh, a_sc, par):
        k.act(junk2[:], xt, AF.Square, accum=ssq[:, si:si + 1])
        k.act(rstd[:, si:si + 1], ssq[:, si:si + 1], AF.Sqrt, bias=eps_t[:, 0:1], scale=1.0 / 1024)
        k.recip(rstd[:, si:si + 1], rstd[:, si:si + 1])
        xnb = xn2[par]
        k.act(xnb[:], xt, AF.Copy, scale=rstd[:, si:si + 1])
        pta = pbf(4 + 2 * par).rearrange("p (a b) -> p a b", b=128)
        ptb = pbf(5 + 2 * par).rearrange("p (a b) -> p a b", b=128)
        for kc in range(8):
            k.tr((pta if kc < 4 else ptb)[:, kc % 4, :], xnb[:, kc * 128:(kc + 1) * 128], ident_b[:])
        for kc in range(8):
            if kc < 4:
                k.ts("dve", dstT[:, kc, tcol:tcol + 128], pta[:, kc, :], modT[:, a_sc, kc, c:c + 1], ALU.mult,
                     modT[:, a_sh, kc, c:c + 1], ALU.add)
            else:
                k.act(dstT[:, kc, tcol:tcol + 128], ptb[:, kc - 4, :], AF.Identity, bias=modT[:, a_sh, kc, c:c + 1],
                      scale=modT[:, a_sc, kc, c:c + 1])

    for t in range(6):
        k.dma("sp", xres[:, t, :], xin[t * 128:(t + 1) * 128, :])
    for t in range(6):
        c = 0 if t < 4 else 1
        for hh in range(2):
            ps = PB[hh][:, :]
            for kc in range(8):
                k.mm(ps, mixT[:, kc, t * 128:(t + 1) * 128], wo[:, kc, hh * 512:(hh + 1) * 512], kc == 0, kc == 7)
            k.tt("dve", ytmp[:, hh * 512:(hh + 1) * 512], ps, gb_row[:, 0, c, hh * 512:(hh + 1) * 512], ALU.mult)
        k.tt("dve", xres[:, t, :], xres[:, t, :], ytmp[:], ALU.add)
        norm_to_T2(xres[:, t, :], t, h2T, t * 128, c, 2, 3, t % 2)
    k.tap("h2T", h2T[:])
    if stage <= 9:
        return k

    S.phase = "G"
    uT = k.sb("uT", [128, 32, NOWN], BF16, off=offA)
    ff1s = [k.sb("ff1s%d" % i, [128, 8, 512], BF16, off=offA + 49152 + 8192 * i) for i in range(2)]
    wf2 = k.sb("wf2", [128, 32, 1024], BF16, off=offA + 65536)
    fnw = k.sb("fnw", [128, 1024], F32, off=soff)
    ytmp2 = k.sb("ytmp2", [128, 1024], F32, off=soff + 4096)
    rtmp = [k.sb("rtmp%d" % i, [128, 512], F32, off=soff + 8192 + 2048 * i) for i in range(2)]
    g1_off = k.base_of[gb_row.name][1]
    outst = [k.sb("outst%d" % i, [128, 1024], F32, off=g1_off + 4096 * i) for i in range(2)]
    k.dma("sp", fnw[:], final_norm_w.partition_broadcast(128))
    rc = 0
    for fb in range(8):
        wb = ff1s[fb % 2]
        k.dma("sp", wb[:], sc1[fb].rearrange("p (kc n) -> p kc n", kc=8))
        k.dma("sp", wf2[:, 4 * fb:4 * fb + 4, :], sc2[fb].rearrange("p (ft n) -> p ft n", ft=4))
        for fi in range(4):
            ftile = fb * 4 + fi
            for (t0, n) in OB:
                ps = nextbank()[:, 0:n]
                for kc in range(8):
                    k.mm(ps, wb[:, kc, fi * 128:(fi + 1) * 128], h2T[:, kc, t0:t0 + n], kc == 0, kc == 7)
                rt = rtmp[rc % 2]; rc += 1
                k.act(rt[:, 0:n], ps, AF.Relu)
                k.tt("dve", uT[:, ftile, t0:t0 + n], rt[:, 0:n], rt[:, 0:n], ALU.mult)
    wadaG = [k.sb("wadaG%d" % i, [128, 8, 256], BF16, off=g1_off + 4096 * i) for i in range(2)]
    mrG = k.sb("mrG", [2, 1024], F32, off=soff + 4096)
    mod_piece(5, wadaG, mrG, baddL, hw=256, banks=(4, 4, 4, 4))
    for t in range(6):
        c = 0 if t < 4 else 1
        for hh in range(2):
            ps = PB[4 + 2 * (t % 2) + hh][:, :]
            for ft in range(32):
                k.mm(ps, uT[:, ft, t * 128:(t + 1) * 128], wf2[:, ft, hh * 512:(hh + 1) * 512], ft == 0, ft == 31)
            k.tt("dve", ytmp2[:, hh * 512:(hh + 1) * 512], ps, gb_row[:, 1, c, hh * 512:(hh + 1) * 512], ALU.mult)
        k.tt("dve", xres[:, t, :], xres[:, t, :], ytmp2[:], ALU.add)
        k.act(junk2[:], xres[:, t, :], AF.Square, accum=ssq[:, 8 + t:9 + t])
        k.act(rstd[:, 8 + t:9 + t], ssq[:, 8 + t:9 + t], AF.Sqrt, bias=eps_t[:, 0:1], scale=1.0 / 1024)
        k.recip(rstd[:, 8 + t:9 + t], rstd[:, 8 + t:9 + t])
        ob_ = outst[t % 2]
        k.stt(ob_[:], xres[:, t, :], rstd[:, 8 + t:9 + t], fnw[:], ALU.mult, ALU.mult)
        k.dma("sp", yo[t * 128:(t + 1) * 128, :], ob_[:])
    return k


def finish(k):
    with k.nc.allow_non_contiguous_dma(reason="param loads"):
        block = k.st.enter_context(k.nc.Block())
        k.S.emit(block)
    k.st.close()
    return k.nc


def make_consts():
    c = np.zeros((128, 1024), np.float32)
    p = np.arange(128)[:, None]; j = np.arange(128)[None, :]
    c[:, 0:128] = (p <= j)
    c[:, 128:256] = (p >= j)
    for d in range(2):
        for pr in range(2):
            blk = np.zeros((128, 128), np.float32)
            for pp in range(128):
                blk[32 * d + 2 * pr + (pp >= 64), pp] = 1.0
            c[:, 256 + (2 * d + pr) * 128: 256 + (2 * d + pr + 1) * 128] = blk
    return c


def make_in_maps(inp):
    f = lambda a: np.ascontiguousarray(np.asarray(a, dtype=np.float32))
    consts = make_consts()
    maps = []
    for core in range(NCORES):
        b, q = core // 4, core % 4
        xs = np.roll(inp["x_sample"][b], -256 * q, axis=0)
        xin = np.concatenate([inp["x_prompt"][2 * core], inp["x_prompt"][2 * core + 1], xs], axis=0)
        mf = np.array([1.0 if (q + j) >= 4 else 0.0 for j in range(1, 4)], np.float32)
        mb = np.array([1.0 if (q + j) < 4 else 0.0 for j in range(1, 4)], np.float32)
        tmask = np.stack([np.repeat(mf, 256), np.repeat(mb, 256)])
        true_row = (4 * q + np.arange(16)) % 16
        mu = (true_row != 0).astype(np.float32)
        md = (true_row != 15).astype(np.float32)
        cmask = np.stack([np.repeat(mu, 64), np.repeat(md, 64)])
        m = {
            "xin": f(xin), "cond": f(np.stack([inp["c_ctx"], inp["c"][b]])),
            "sgla": f(inp["state_gla"][b, 0]), "sC": f(inp["state_mlstm_C"][b, 0]),
            "sn": f(inp["state_mlstm_n"][b, 0]), "sm": f(inp["state_mlstm_m"][b, 0]),
            "tmask": f(tmask), "tmaskT": f(tmask.reshape(2, 6, 128).transpose(2, 1, 0)), "cmask": f(cmask), "consts": consts,
            "w_ada": f(inp["w_ada"][0]), "b_ada": f(inp["b_ada"][0]),
            "norm1_w": f(inp["norm1_w"][0]), "norm2_w": f(inp["norm2_w"][0]),
            "w_in": f(inp["w_in"][0]), "w_alpha2": f(inp["w_alpha2"][0]), "b_alpha": f(inp["b_alpha"][0]),
            "b_mgate": f(inp["b_mgate"][0]), "conv_w": f(inp["conv_w"][0]),
            "gnorm_a_w": f(inp["gnorm_a_w"][0]), "gnorm_b_w": f(inp["gnorm_b_w"][0]),
            "w_out": f(inp["w_out"][0]), "w_ff1": f(inp["w_ff1"][0]), "w_ff2": f(inp["w_ff2"][0]),
            "final_norm_w": f(inp["final_norm_w"]),
        }
        maps.append(m)
    return maps


_NC_CACHE = {}


def kernel(**inp):
    if "nc" not in _NC_CACHE:
        _NC_CACHE["nc"] = finish(build())
    nc = _NC_CACHE["nc"]
    maps = make_in_maps(inp)
    res = run_bass_kernel_spmd(nc, maps, core_ids=list(range(NCORES))).results
    y_prompt = np.zeros((16, 256, 1024), np.float32)
    y_sample = np.zeros((2, 1024, 1024), np.float32)
    nS = np.zeros((16, 1, 2, 4, 64, 128), np.float32); nC = np.zeros((16, 1, 2, 4, 64, 128), np.float32)
    nn = np.zeros((16, 1, 2, 4, 64), np.float32); nm = np.zeros((16, 1, 2, 4), np.float32)
    for core in range(NCORES):
        r = res[core]
        b, q = core // 4, core % 4
        y = np.asarray(r["yo"], np.float32)
        y_prompt[2 * core] = y[0:256]; y_prompt[2 * core + 1] = y[256:512]
        y_sample[b, 256 * q:256 * q + 256] = y[512:768]
        nS[2 * core:2 * core + 2, 0] = np.asarray(r["oS"]); nC[2 * core:2 * core + 2, 0] = np.asarray(r["oC"])
        nn[2 * core:2 * core + 2, 0] = np.asarray(r["on"]); nm[2 * core:2 * core + 2, 0] = np.asarray(r["om"])
    return (y_prompt, y_sample, nS, nC, nn, nm)
```
